# Optimizing a Trainium2 kernel written in Bass

```python
import math
import jax, jax.numpy as jnp
from jax import lax
import numpy as np

D_MODEL = 1024
BATCH = 2
SEQ = 8192
DEPTH = 4

HEAD_DIM = 64
H_FOX = 8
H_SB = 8
N_HEADS = H_FOX + H_SB
MIX_WIDTH = N_HEADS * HEAD_DIM
ATTN_IN = 3 * MIX_WIDTH + H_FOX
CONV_WIDTH = D_MODEL
CONV_K = 3
FFN_CONV_K = 3
D_FF = 2816
Q_BLOCK = 128
N_ATTN = (DEPTH + 1) // 2
N_CONV = DEPTH // 2
EPS = 1e-6

kernel_name = 'fox_stickbreak_shortconv_hybrid'


def rms_norm(x, g):
    xf = x.astype(jnp.float32)
    y = xf * lax.rsqrt(jnp.mean(xf * xf, axis=-1, keepdims=True) + EPS)
    return (y * g.astype(jnp.float32)).astype(x.dtype)


def causal_dwconv(x, w):
    k = w.shape[0]
    s = x.shape[1]
    xp = jnp.pad(x, ((0, 0), (k - 1, 0), (0, 0)))
    return sum(xp[:, j:j + s] * w[j] for j in range(k))


def split_query_blocks(t):
    b, h, s = t.shape[:3]
    rest = t.shape[3:]
    t = t.reshape((b, h, s // Q_BLOCK, Q_BLOCK) + rest)
    return jnp.moveaxis(t, 2, 0)


def merge_query_blocks(t):
    nb, b, h, qb, d = t.shape
    return jnp.moveaxis(t, 0, 2).reshape(b, h, nb * qb, d)


def forgetting_attention(q, k, v, log_f):
    s, dh = q.shape[2], q.shape[3]
    cum_f = jnp.cumsum(log_f, axis=-1)
    key_pos = jnp.arange(s)
    scale = dh ** -0.5
    nb = s // Q_BLOCK

    def one_block(args):
        i, qi, fi = args
        q_pos = i * Q_BLOCK + jnp.arange(Q_BLOCK)
        logits = jnp.einsum('bhqd,bhkd->bhqk', qi, k,
                            preferred_element_type=jnp.float32) * scale
        logits = logits + fi[..., :, None] - cum_f[:, :, None, :]
        causal = key_pos[None, :] <= q_pos[:, None]
        logits = jnp.where(causal, logits, -jnp.inf)
        p = jax.nn.softmax(logits, axis=-1)
        return jnp.einsum('bhqk,bhkd->bhqd', p.astype(v.dtype), v)

    out = lax.map(one_block, (jnp.arange(nb), split_query_blocks(q), split_query_blocks(cum_f)))
    return merge_query_blocks(out)


def stick_breaking_attention(q, k, v):
    s, dh = q.shape[2], q.shape[3]
    key_pos = jnp.arange(s)
    scale = dh ** -0.5
    nb = s // Q_BLOCK

    def one_block(args):
        i, qi = args
        q_pos = i * Q_BLOCK + jnp.arange(Q_BLOCK)
        z = jnp.einsum('bhqd,bhkd->bhqk', qi, k,
                       preferred_element_type=jnp.float32) * scale
        strict = key_pos[None, :] < q_pos[:, None]
        log_beta = jax.nn.log_sigmoid(z)
        log_one_minus = jnp.where(strict, jax.nn.log_sigmoid(-z), 0.0)
        key_axis = log_one_minus.ndim - 1
        later = lax.cumsum(log_one_minus, axis=key_axis, reverse=True) - log_one_minus
        w = jnp.where(strict, jnp.exp(log_beta + later), 0.0)
        return jnp.einsum('bhqk,bhkd->bhqd', w.astype(v.dtype), v)

    out = lax.map(one_block, (jnp.arange(nb), split_query_blocks(q)))
    return merge_query_blocks(out)


def attention_mixer(h, norm_g, w_in, f_bias, fox_q_g, fox_k_g, sb_q_g, sb_k_g, w_out):
    b, s, _ = h.shape
    xn = rms_norm(h, norm_g)
    proj = xn @ w_in

    def heads(t):
        return t.reshape(b, s, N_HEADS, HEAD_DIM).transpose(0, 2, 1, 3)

    q = heads(proj[..., :MIX_WIDTH])
    k = heads(proj[..., MIX_WIDTH:2 * MIX_WIDTH])
    v = heads(proj[..., 2 * MIX_WIDTH:3 * MIX_WIDTH])
    f_logit = proj[..., 3 * MIX_WIDTH:].astype(jnp.float32) + f_bias.astype(jnp.float32)
    log_f = jax.nn.log_sigmoid(f_logit).transpose(0, 2, 1)

    q_fox = rms_norm(q[:, :H_FOX], fox_q_g)
    k_fox = rms_norm(k[:, :H_FOX], fox_k_g)
    q_sb = rms_norm(q[:, H_FOX:], sb_q_g)
    k_sb = rms_norm(k[:, H_FOX:], sb_k_g)

    o_fox = forgetting_attention(q_fox, k_fox, v[:, :H_FOX], log_f)
    o_sb = stick_breaking_attention(q_sb, k_sb, v[:, H_FOX:])
    o = jnp.concatenate([o_fox, o_sb], axis=1)
    o = o.transpose(0, 2, 1, 3).reshape(b, s, MIX_WIDTH)
    return o @ w_out


def short_conv_mixer(h, norm_g, w_in, conv_w, w_out):
    xn = rms_norm(h, norm_g)
    proj = xn @ w_in
    gate_b = proj[..., :CONV_WIDTH]
    gate_c = proj[..., CONV_WIDTH:2 * CONV_WIDTH]
    u = proj[..., 2 * CONV_WIDTH:]
    y = gate_b * causal_dwconv(gate_c * u, conv_w)
    return y @ w_out


def conv_ffn(h, norm_g, w_up, conv_w, w_down):
    xn = rms_norm(h, norm_g)
    u = causal_dwconv(xn @ w_up, conv_w)
    g, val = u[..., :D_FF], u[..., D_FF:]
    return (jax.nn.silu(g) * val) @ w_down


def setup_inputs(seed: int = 0) -> dict:
    key = jax.random.key(seed)
    ks = jax.random.split(key, 20)
    f32 = jnp.float32
    out_scale = (2 * DEPTH) ** -0.5

    def normal(k, shape, scale):
        return scale * jax.random.normal(k, shape, f32)

    def gain(k, shape):
        return 1.0 + 0.05 * jax.random.normal(k, shape, f32)

    x = jax.random.normal(ks[0], (BATCH, SEQ, D_MODEL), f32)
    attn_norm = gain(ks[1], (N_ATTN, D_MODEL))
    attn_w_in = normal(ks[2], (N_ATTN, D_MODEL, ATTN_IN), D_MODEL ** -0.5)
    attn_f_bias = jnp.linspace(1.0, 6.0, H_FOX, dtype=f32)[None, :] + normal(ks[3], (N_ATTN, H_FOX), 0.1)
    fox_q_gain = gain(ks[4], (N_ATTN, HEAD_DIM))
    fox_k_gain = gain(ks[5], (N_ATTN, HEAD_DIM))
    sb_q_gain = gain(ks[6], (N_ATTN, HEAD_DIM))
    sb_k_gain = gain(ks[7], (N_ATTN, HEAD_DIM))
    attn_w_out = normal(ks[8], (N_ATTN, MIX_WIDTH, D_MODEL), out_scale * MIX_WIDTH ** -0.5)
    conv_norm = gain(ks[9], (N_CONV, D_MODEL))
    conv_w_in = normal(ks[10], (N_CONV, D_MODEL, 3 * CONV_WIDTH), D_MODEL ** -0.5)
    conv_kernel = normal(ks[11], (N_CONV, CONV_K, CONV_WIDTH), CONV_K ** -0.5)
    conv_w_out = normal(ks[12], (N_CONV, CONV_WIDTH, D_MODEL), out_scale * CONV_WIDTH ** -0.5)
    ffn_norm = gain(ks[13], (DEPTH, D_MODEL))
    ffn_w_up = normal(ks[14], (DEPTH, D_MODEL, 2 * D_FF), D_MODEL ** -0.5)
    ffn_conv = normal(ks[15], (DEPTH, FFN_CONV_K, 2 * D_FF), FFN_CONV_K ** -0.5)
    ffn_w_down = normal(ks[16], (DEPTH, D_FF, D_MODEL), out_scale * D_FF ** -0.5)
    return {
        'x': x,
        'attn_norm': attn_norm, 'attn_w_in': attn_w_in, 'attn_f_bias': attn_f_bias,
        'fox_q_gain': fox_q_gain, 'fox_k_gain': fox_k_gain,
        'sb_q_gain': sb_q_gain, 'sb_k_gain': sb_k_gain, 'attn_w_out': attn_w_out,
        'conv_norm': conv_norm, 'conv_w_in': conv_w_in, 'conv_kernel': conv_kernel,
        'conv_w_out': conv_w_out,
        'ffn_norm': ffn_norm, 'ffn_w_up': ffn_w_up, 'ffn_conv': ffn_conv, 'ffn_w_down': ffn_w_down,
    }


def reference(x, attn_norm, attn_w_in, attn_f_bias, fox_q_gain, fox_k_gain, sb_q_gain,
              sb_k_gain, attn_w_out, conv_norm, conv_w_in, conv_kernel, conv_w_out,
              ffn_norm, ffn_w_up, ffn_conv, ffn_w_down):
    h = x
    for layer in range(DEPTH):
        i = layer // 2
        if layer % 2 == 0:
            h = h + attention_mixer(h, attn_norm[i], attn_w_in[i], attn_f_bias[i],
                                    fox_q_gain[i], fox_k_gain[i], sb_q_gain[i], sb_k_gain[i],
                                    attn_w_out[i])
        else:
            h = h + short_conv_mixer(h, conv_norm[i], conv_w_in[i], conv_kernel[i], conv_w_out[i])
        h = h + conv_ffn(h, ffn_norm[layer], ffn_w_up[layer], ffn_conv[layer], ffn_w_down[layer])
    return h
```

```python
import numpy as np
import ml_dtypes
from contextlib import ExitStack
import concourse.bass as bass
import concourse.mybir as mybir
from concourse.bass_utils import run_bass_kernel_spmd

F32 = mybir.dt.float32
BF16 = mybir.dt.bfloat16
AF = mybir.ActivationFunctionType
ALU = mybir.AluOpType

D = 1024
S = 8192
NB = 2
DFF = 2816
EPS = 1e-6
NTOK = 2048
PAD = 8
NCOL = NTOK + PAD
TILES = [(2, 346), (346, 690), (690, 1032), (1032, 1374), (1374, 1716), (1716, 2056)]
GROUPS = [(0, 1, 2), (3, 4, 5)]
GBASE = [2, 1032]
GW = 1032
A_LOFF = [0, 344, 688]
TW = 346
NPS = 8


class Prog:
    ENGS = ("sp", "pe", "act", "dve", "pool")

    def __init__(self, nc, stack):
        self.nc = nc
        self.stack = stack
        self.streams = {e: [] for e in self.ENGS}
        self.esem = {}
        self.ecnt = {}
        self.sid = 0
        for e in ("pe", "act", "dve", "pool"):
            self.esem[e] = (self._sid(), stack.enter_context(nc.semaphore("s_" + e)))
            self.ecnt[e] = 0
        self.waited = {e: {} for e in self.ENGS}
        self.res = {}
        self.dsem = {}
        self.ps_i = 0
        self.tmp_i = 0

    def _sid(self):
        self.sid += 1
        return self.sid

    def _deps(self, eng, reads, writes):
        need = {}
        wd = self.waited[eng]

        def add(ev):
            if ev is None:
                return
            sid, sem, val, src = ev
            if eng == "pe" and src == "pe":
                return
            if wd.get(sid, 0) >= val:
                return
            if sid not in need or need[sid][1] < val:
                need[sid] = (sem, val)

        for r in reads:
            st = self.res.get(r)
            if st is not None:
                add(st[0])
        for w in writes:
            st = self.res.get(w)
            if st is not None:
                add(st[0])
                for ev in st[1].values():
                    add(ev)
        for sid, (sem, val) in need.items():
            wd[sid] = val
        return list(need.values())

    def _record(self, ev, reads, writes):
        for w in writes:
            self.res[w] = [ev, {}]
        ws = set(writes)
        for r in reads:
            if r in ws:
                continue
            st = self.res.get(r)
            if st is None:
                st = [None, {}]
                self.res[r] = st
            st[1][ev[0]] = ev

    def op(self, eng, fn, reads=(), writes=()):
        waits = self._deps(eng, reads, writes)
        self.ecnt[eng] += 1
        sid, sem = self.esem[eng]
        ev = (sid, sem, self.ecnt[eng], eng)
        self.streams[eng].append((waits, fn, (sem, 1)))
        self._record(ev, reads, writes)

    def dma(self, q, out, in_, reads=(), writes=(), semkey=None):
        waits = self._deps(q, reads, writes)
        key = semkey if semkey is not None else (writes[0] if writes else ("st", reads[0]))
        ent = self.dsem.get(key)
        if ent is None:
            ent = [self._sid(), self.stack.enter_context(self.nc.semaphore("d%d" % len(self.dsem))), 0]
            self.dsem[key] = ent
        ent[2] += 16
        ev = (ent[0], ent[1], ent[2], "dma")
        self.streams[q].append((waits, lambda e: e.dma_start(out=out, in_=in_), (ent[1], 16)))
        self._record(ev, reads, writes)

    def mm(self, out, lhsT, rhs, start, stop, reads, writes):
        self.op("pe", lambda e: e.matmul(out, lhsT=lhsT, rhs=rhs, start=start, stop=stop), reads, writes)

    def act(self, out, in_, func, reads, writes, **kw):
        self.op("act", lambda e: e.activation(out=out, in_=in_, func=func, **kw), reads, writes)

    def stt(self, out, in0, scalar, in1, op0, op1, reads, writes, eng="dve"):
        self.op(eng, lambda e: e.scalar_tensor_tensor(out=out, in0=in0, scalar=scalar, in1=in1, op0=op0, op1=op1),
                reads, writes)

    def tt(self, out, in0, in1, op, reads, writes, eng="dve"):
        self.op(eng, lambda e: e.tensor_tensor(out=out, in0=in0, in1=in1, op=op), reads, writes)

    def ts(self, out, in0, s1, s2, op0, op1, reads, writes, eng="dve"):
        if op1 is None:
            self.op(eng, lambda e: e.tensor_scalar(out=out, in0=in0, scalar1=s1, scalar2=None, op0=op0), reads, writes)
        else:
            self.op(eng, lambda e: e.tensor_scalar(out=out, in0=in0, scalar1=s1, scalar2=s2, op0=op0, op1=op1),
                    reads, writes)

    def recip(self, out, in_, reads, writes):
        self.op("dve", lambda e: e.reciprocal(out=out, in_=in_), reads, writes)

    def copy(self, eng, out, in_, reads, writes):
        self.op(eng, lambda e: e.tensor_copy(out=out, in_=in_), reads, writes)

    def memset(self, eng, ap, val, writes):
        self.op(eng, lambda e: e.memset(ap, val), (), writes)

    def emit(self):
        nc = self.nc
        fin = [(ent[1], ent[2]) for ent in self.dsem.values()]
        with nc.Block() as block:
            for eng, reg in (("sp", block.sync), ("pe", block.tensor), ("act", block.scalar),
                             ("dve", block.vector), ("pool", block.gpsimd)):
                items = self.streams[eng]

                def body(e, items=items, eng=eng):
                    for waits, fn, inc in items:
                        for sem, val in waits:
                            e.wait_ge(sem, val)
                        fn(e).then_inc(inc[0], inc[1])
                    if eng == "sp":
                        for sem, val in fin:
                            e.wait_ge(sem, val)

                reg(body)


class TCtx:
    pass


def _wview(W, c0, w):
    return W.rearrange("(k p) c -> p k c", p=128)[:, :, c0:c0 + w]


def t_alloc(nc, stack, P):
    C = TCtx()
    C.P = P
    sb = lambda name, shape, dt: stack.enter_context(nc.sbuf_tensor(name, shape, dt))
    C.hT = sb("hT", [128, 8, NCOL], F32)
    C.xn = sb("xn", [128, 8, NCOL], BF16)
    C.a = sb("abuf", [128, 22, GW], BF16)
    C.wbuf = [sb("wbuf%d" % i, [128, 6144], BF16) for i in range(2)]
    C.sq = [sb("sq%d" % i, [128, 8, TW], BF16) for i in range(2)]
    C.tmp = [sb("tmp%d" % i, [128, TW], F32) for i in range(8)]
    C.stg = [sb("stg%d" % i, [128, 512], BF16) for i in range(6)]
    C.lf = [sb("lf%d" % i, [8, TW], F32) for i in range(6)]
    C.ones = sb("ones_bf", [128, 128], BF16)
    C.bd = sb("bd_bf", [128, 128], BF16)
    C.ps = [stack.enter_context(nc.psum_tensor("ps%d" % i, [128, 512], F32)) for i in range(NPS)]
    C.wi = 0
    C.sqi = 0
    C.tmpi = 0
    C.psi = 0
    C.stgi = 0
    C.lfi = 0
    return C


def _ps(C):
    i = C.psi
    C.psi = (i + 1) % NPS
    return C.ps[i], ("ps", i)


def _tmp(C):
    i = C.tmpi
    C.tmpi = (i + 1) % len(C.tmp)
    return C.tmp[i], ("tmp", i)


def _stg(C):
    i = C.stgi
    C.stgi = (i + 1) % len(C.stg)
    return C.stg[i], ("stg", i)


def _lf(C):
    i = C.lfi
    C.lfi = (i + 1) % len(C.lf)
    return C.lf[i], ("lf", i)


def _loadw(C, parts, kdim):
    P = C.P
    i = C.wi
    C.wi = (i + 1) % 2
    wt = C.wbuf[i]
    tot = sum(w for _, _, w in parts)
    view = wt[:, 0:kdim * tot].rearrange("p (k c) -> p k c", k=kdim)
    off = 0
    for W, c0, w in parts:
        P.dma("pool", view[:, :, off:off + w], _wview(W, c0, w), reads=(), writes=[("w", i)], semkey=("w", i))
        off += w
    return view, ("w", i)


def t_norm(C, gcol, tiles=range(6)):
    P = C.P
    for ti in tiles:
        s, e = TILES[ti]
        s0 = 0 if ti == 0 else s
        n = e - s0
        sq = C.sq[C.sqi]
        sqk = ("sq", C.sqi)
        C.sqi = (C.sqi + 1) % 2
        hk = [("h", c, ti) for c in range(8)]
        P.act(sq[:, :, 0:n], C.hT[:, :, s0:e], AF.Square, reads=hk, writes=[sqk])
        ps, psk = _ps(C)
        for c in range(8):
            P.mm(ps[:, 0:n], C.ones[:, :], sq[:, c, 0:n], c == 0, c == 7, reads=[sqk, "const"], writes=[psk])
        rt, rtk = _tmp(C)
        P.act(rt[:, 0:n], ps[:, 0:n], AF.Sqrt, reads=[psk], writes=[rtk], scale=1.0 / D, bias=C.eps1[:, 0:1])
        P.recip(rt[:, 0:n], rt[:, 0:n], reads=[rtk], writes=[rtk])
        for c in range(8):
            P.stt(C.xn[:, c, s0:e], C.hT[:, c, s0:e], gcol[:, c:c + 1], rt[:, 0:n], ALU.mult, ALU.mult,
                  reads=[("h", c, ti), rtk, "const"], writes=[("xn", c, ti)])


def _xnk(ti, halo=True):
    ks = [("xn", c, ti) for c in range(8)]
    if halo and ti > 0:
        ks += [("xn", c, ti - 1) for c in range(8)]
    return ks


def _conv3(C, ps, psk, n, wcol, j):
    P = C.P
    t, tk = _tmp(C)
    P.act(t[:, 0:n], ps[:, 2:n + 2], AF.Identity, reads=[psk, "const"], writes=[tk], scale=wcol[:, j, 2:3])
    P.stt(t[:, 0:n], ps[:, 1:n + 1], wcol[:, j, 1:2], t[:, 0:n], ALU.mult, ALU.add, reads=[psk, tk, "const"], writes=[tk])
    P.stt(t[:, 0:n], ps[:, 0:n], wcol[:, j, 0:1], t[:, 0:n], ALU.mult, ALU.add, reads=[psk, tk, "const"], writes=[tk])
    return t, tk


def t_proj_down(C, W, kdim, gi):
    P = C.P
    for ob in range(4):
        wt, wk = _loadw(C, [(W, 256 * ob, 256)], kdim)
        for ti in GROUPS[gi]:
            s, e = TILES[ti]
            n = e - s
            ls = A_LOFF[ti % 3]
            for jj in range(2):
                jc = 2 * ob + jj
                ps, psk = _ps(C)
                for k in range(kdim):
                    P.mm(ps[:, 0:n], wt[:, k, jj * 128:(jj + 1) * 128], C.a[:, k, ls:ls + n], k == 0, k == kdim - 1,
                         reads=[wk, ("a", k, ti % 3)], writes=[psk])
                P.tt(C.hT[:, jc, s:e], ps[:, 0:n], C.hT[:, jc, s:e], ALU.add, reads=[psk, ("h", jc, ti)],
                     writes=[("h", jc, ti)])


def t_outproj(C, oT, W):
    P = C.P
    for gi in range(2):
        for ti in GROUPS[gi]:
            s, e = TILES[ti]
            ls = A_LOFF[ti % 3]
            P.dma("sp", C.a[:, 0:8, ls:ls + (e - s)], oT.rearrange("(k p) c -> p k c", p=128)[:, :, s:e],
                  reads=(), writes=[("a", k, ti % 3) for k in range(8)], semkey=("aload", ti))
        t_proj_down(C, W, 8, gi)


def t_ffn(C, gcol, W_up, W_dn, cw):
    P = C.P
    t_norm(C, gcol)
    for gi in range(2):
        for ub in range(11):
            wt, wk = _loadw(C, [(W_up, 256 * ub, 256), (W_up, DFF + 256 * ub, 256)], 8)
            for ti in GROUPS[gi]:
                s, e = TILES[ti]
                n = e - s
                ls = A_LOFF[ti % 3]
                xk = _xnk(ti)
                for pj in range(2):
                    i = 2 * ub + pj
                    gp, gk = _ps(C)
                    vp, vk = _ps(C)
                    for kc in range(8):
                        P.mm(gp[:, 0:n + 2], wt[:, kc, pj * 128:(pj + 1) * 128], C.xn[:, kc, s - 2:e], kc == 0, kc == 7,
                             reads=[wk] + xk, writes=[gk])
                    for kc in range(8):
                        P.mm(vp[:, 0:n + 2], wt[:, kc, 256 + pj * 128:256 + (pj + 1) * 128], C.xn[:, kc, s - 2:e],
                             kc == 0, kc == 7, reads=[wk] + xk, writes=[vk])
                    t1, t1k = _conv3(C, gp, gk, n, cw, i)
                    t2, t2k = _conv3(C, vp, vk, n, cw, 22 + i)
                    P.act(t1[:, 0:n], t1[:, 0:n], AF.Silu, reads=[t1k], writes=[t1k])
                    P.tt(C.a[:, i, ls:ls + n], t1[:, 0:n], t2[:, 0:n], ALU.mult, reads=[t1k, t2k],
                         writes=[("a", i, ti % 3)])
        t_proj_down(C, W_dn, 22, gi)


def t_convmix(C, gcol, W_in, ck, W_out):
    P = C.P
    t_norm(C, gcol)
    for gi in range(2):
        for cb in range(4):
            wt, wk = _loadw(C, [(W_in, 256 * cb, 256), (W_in, 1024 + 256 * cb, 256), (W_in, 2048 + 256 * cb, 256)], 8)
            for ti in GROUPS[gi]:
                s, e = TILES[ti]
                n = e - s
                ls = A_LOFF[ti % 3]
                xk = _xnk(ti)
                for cj in range(2):
                    i = 2 * cb + cj
                    pss = []
                    for part in range(3):
                        pp, pk = _ps(C)
                        for kc in range(8):
                            P.mm(pp[:, 0:n + 2], wt[:, kc, 256 * part + cj * 128:256 * part + (cj + 1) * 128],
                                 C.xn[:, kc, s - 2:e], kc == 0, kc == 7, reads=[wk] + xk, writes=[pk])
                        pss.append((pp, pk))
                    (bp, bk), (cp, ck_), (up, uk) = pss
                    tc_, tck = _tmp(C)
                    P.act(tc_[:, 0:n + 2], cp[:, 0:n + 2], AF.Copy, reads=[ck_], writes=[tck])
                    P.tt(tc_[:, 0:n + 2], up[:, 0:n + 2], tc_[:, 0:n + 2], ALU.mult, reads=[uk, tck], writes=[tck])
                    t3, t3k = _conv3(C, tc_, tck, n, ck, i)
                    P.tt(C.a[:, i, ls:ls + n], bp[:, 2:n + 2], t3[:, 0:n], ALU.mult, reads=[bk, t3k],
                         writes=[("a", i, ti % 3)])
        t_proj_down(C, W_out, 8, gi)


def t_qkv(C, gcol, W_in, gqk, fb, wf, QT, KT, V, LF):
    P = C.P
    t_norm(C, gcol)
    for gi in range(2):
        for qb in range(8):
            wt, wk = _loadw(C, [(W_in, 256 * qb, 256)], 8)
            for ti in GROUPS[gi]:
                s, e = TILES[ti]
                n = e - s
                xk = _xnk(ti, halo=False)
                so = max(s, PAD)
                for cj in range(2):
                    c = 2 * qb + cj
                    isq = c < 8
                    pp, pk = _ps(C)
                    for kc in range(8):
                        P.mm(pp[:, 0:n], wt[:, kc, cj * 128:(cj + 1) * 128], C.xn[:, kc, s:e], kc == 0, kc == 7,
                             reads=[wk] + xk, writes=[pk])
                    sq, sqk = _stg(C)
                    P.act(sq[:, 0:n], pp[:, 0:n], AF.Square, reads=[pk], writes=[sqk])
                    p2, p2k = _ps(C)
                    P.mm(p2[:, 0:n], C.bd[:, :], sq[:, 0:n], True, True, reads=[sqk, "const"], writes=[p2k])
                    rt, rtk = _tmp(C)
                    if isq:
                        P.act(rt[:, 0:n], p2[:, 0:n], AF.Sqrt, reads=[p2k], writes=[rtk], scale=1.0, bias=C.eps64[:, 0:1])
                    else:
                        P.act(rt[:, 0:n], p2[:, 0:n], AF.Sqrt, reads=[p2k], writes=[rtk], scale=1.0 / 64, bias=C.eps1[:, 0:1])
                    P.recip(rt[:, 0:n], rt[:, 0:n], reads=[rtk], writes=[rtk])
                    st, stk = _stg(C)
                    P.stt(st[:, 0:n], pp[:, 0:n], gqk[:, c:c + 1], rt[:, 0:n], ALU.mult, ALU.mult,
                          reads=[pk, rtk, "const"], writes=[stk])
                    dst = QT if isq else KT
                    cc = c % 8
                    P.dma("sp", dst[128 * cc:128 * (cc + 1), so - PAD:e - PAD], st[:, so - s:n], reads=[stk], writes=())
        for vb in range(2):
            wt, wk = _loadw(C, [(W_in, 2048 + 512 * vb, 512)], 8)
            for m in range(8):
                c0 = PAD + 1024 * gi + 128 * m
                tis = sorted({ti for ti in range(6) if TILES[ti][0] < c0 + 128 and TILES[ti][1] > c0})
                xk = [("xn", c, ti) for c in range(8) for ti in tis]
                pp, pk = _ps(C)
                for kc in range(8):
                    P.mm(pp[:, 0:512], C.xn[:, kc, c0:c0 + 128], wt[:, kc, 0:512], kc == 0, kc == 7, reads=[wk] + xk,
                         writes=[pk])
                st, stk = _stg(C)
                P.act(st[:, 0:512], pp[:, 0:512], AF.Copy, reads=[pk], writes=[stk])
                r0 = 1024 * gi + 128 * m
                P.dma("sp", V[r0:r0 + 128, 512 * vb:512 * (vb + 1)], st[:, 0:512], reads=[stk], writes=())
        for ti in GROUPS[gi]:
            s, e = TILES[ti]
            n = e - s
            so = max(s, PAD)
            xk = _xnk(ti, halo=False)
            pp, pk = _ps(C)
            for kc in range(8):
                P.mm(pp[0:8, 0:n], wf[:, kc, :], C.xn[:, kc, s:e], kc == 0, kc == 7, reads=["const"] + xk, writes=[pk])
            y, yk = _lf(C)
            P.act(y[:, 0:n], pp[0:8, 0:n], AF.Identity, reads=[pk, "const"], writes=[yk], bias=fb[:, 0:1], scale=1.0)
            ay, ayk = _lf(C)
            P.act(ay[:, 0:n], y[:, 0:n], AF.Abs, reads=[yk], writes=[ayk])
            P.act(ay[:, 0:n], ay[:, 0:n], AF.Exp, reads=[ayk], writes=[ayk], scale=-1.0)
            P.act(ay[:, 0:n], ay[:, 0:n], AF.Ln, reads=[ayk], writes=[ayk], bias=C.one1[0:8, 0:1], scale=1.0)
            P.ts(y[:, 0:n], y[:, 0:n], 0.0, None, ALU.min, None, reads=[yk], writes=[yk])
            P.tt(y[:, 0:n], y[:, 0:n], ay[:, 0:n], ALU.subtract, reads=[yk, ayk], writes=[yk])
            P.dma("sp", LF[0:8, so - PAD:e - PAD], y[:, so - s:n], reads=[yk], writes=())


def build_T(mode, upto=9):
    nc = bass.Bass("TRN2", target_bir_lowering=False)
    din = lambda name, shape, dt=F32: nc.dram_tensor(name, shape, dt, kind="ExternalInput").ap()
    dout = lambda name, shape, dt=F32: nc.dram_tensor(name, shape, dt, kind="ExternalOutput").ap()
    hin = din("hin", [D, NCOL])
    consts = din("cst_bf", [128, 256], BF16)
    if mode in (2, 3):
        oT = din("oT", [D, NCOL], BF16)
        w_o = din("w_o", [D, D])
        ffn = []
        for j in range(2):
            ffn.append(dict(g=din("ffn_g%d" % j, [128, 8]), up=din("ffn_up%d" % j, [D, 2 * DFF]),
                            cw=din("ffn_cw%d" % j, [128, 44, 3]), dn=din("ffn_dn%d" % j, [DFF, D])))
        cv = dict(g=din("cv_g", [128, 8]), w_in=din("cv_in", [D, 3 * D]), ck=din("cv_k", [128, 8, 3]),
                  w_out=din("cv_out", [D, D]))
        hout = dout("hout", [D, NTOK])
    if mode in (1, 2):
        at = dict(g=din("at_g", [128, 8]), w_in=din("at_in", [D, 3080]), gqk=din("at_gqk", [128, 16]),
                  fb=din("at_fb", [8, 1]))
        QT = dout("QT", [D, NTOK], BF16)
        KT = dout("KT", [D, NTOK], BF16)
        V = dout("V", [NTOK, D], BF16)
        LF = dout("LF", [8, NTOK])

    with ExitStack() as stack:
        P = Prog(nc, stack)
        C = t_alloc(nc, stack, P)
        sb = lambda name, shape, dt: stack.enter_context(nc.sbuf_tensor(name, shape, dt))
        C.eps1 = sb("eps1", [128, 1], F32)
        C.eps64 = sb("eps64", [128, 1], F32)
        C.one1 = sb("one1", [128, 1], F32)
        P.memset("dve", C.eps1[:, :], EPS, ["const"])
        P.memset("dve", C.eps64[:, :], 64 * EPS, ["const"])
        P.memset("dve", C.one1[:, :], 1.0, ["const"])
        P.dma("sp", C.ones[:, :], consts[:, 0:128], writes=["const"], semkey="c0")
        P.dma("sp", C.bd[:, :], consts[:, 128:256], writes=["const"], semkey="c1")

        def small(name, ap, shape, dt=F32, q="sp"):
            t = sb(name, shape, dt)
            P.dma(q, t[tuple(slice(None) for _ in shape)], ap, writes=["const"], semkey="c_" + name)
            return t

        if mode in (2, 3):
            for j in range(2):
                ffn[j]["g_sb"] = small("ffn_g_sb%d" % j, ffn[j]["g"], [128, 8])
                ffn[j]["cw_sb"] = small("ffn_cw_sb%d" % j, ffn[j]["cw"], [128, 44, 3])
            cv["g_sb"] = small("cv_g_sb", cv["g"], [128, 8])
            cv["ck_sb"] = small("cv_k_sb", cv["ck"], [128, 8, 3])
        if mode in (1, 2):
            at["g_sb"] = small("at_g_sb", at["g"], [128, 8])
            at["gqk_sb"] = small("at_gqk_sb", at["gqk"], [128, 16])
            at["fb_sb"] = small("at_fb_sb", at["fb"], [8, 1])
            at["wf_sb"] = small("at_wf_sb", at["w_in"].rearrange("(k p) c -> p k c", p=128)[:, :, 3072:3080],
                                [128, 8, 8], BF16, q="pool")
        for c in range(8):
            P.dma("sp", C.hT[:, c, :], hin[128 * c:128 * (c + 1), :], writes=[("h", c, ti) for ti in range(6)],
                  semkey=("hload", c))
        if mode in (2, 3):
            t_outproj(C, oT, w_o)
            if upto >= 2:
                t_ffn(C, ffn[0]["g_sb"], ffn[0]["up"], ffn[0]["dn"], ffn[0]["cw_sb"])
            if upto >= 3:
                t_convmix(C, cv["g_sb"], cv["w_in"], cv["ck_sb"], cv["w_out"])
            if upto >= 4:
                t_ffn(C, ffn[1]["g_sb"], ffn[1]["up"], ffn[1]["dn"], ffn[1]["cw_sb"])
            for c in range(8):
                P.dma("sp", hout[128 * c:128 * (c + 1), :], C.hT[:, c, PAD:NCOL], reads=[("h", c, ti) for ti in range(6)],
                      semkey=("hstore", c))
        if mode in (1, 2):
            t_qkv(C, at["g_sb"], at["w_in"], at["gqk_sb"], at["fb_sb"], at["wf_sb"], QT, KT, V, LF)
        P.emit()
    return nc


QT_W = 512
NQT = S // QT_W
NKB = S // 128
FCH = 512
NFC = S // FCH
ND_WARM = 3
D2, D3 = 2, 4


def build_A():
    nc = bass.Bass("TRN2", target_bir_lowering=False)
    din = lambda name, shape, dt=F32: nc.dram_tensor(name, shape, dt, kind="ExternalInput").ap()
    qf = din("qf", [128, S], BF16)
    kf = din("kf", [128, S], BF16)
    vf = din("vf", [S, 128], BF16)
    lf = din("lf", [2, S])
    qs = din("qs", [128, S], BF16)
    ks = din("ks", [128, S], BF16)
    vs = din("vs", [S, 128], BF16)
    cbf = din("acst_bf", [128, 256], BF16)
    cf32 = din("acst_f32", [128, 2 * 4 * 512])
    ctri = din("acst_tri", [32, 32])
    oT = nc.dram_tensor("oT", [256, S], BF16, kind="ExternalOutput").ap()

    with ExitStack() as stack:
        P = Prog(nc, stack)
        sb = lambda name, shape, dt: stack.enter_context(nc.sbuf_tensor(name, shape, dt))
        Qa = [sb("Qa%d" % i, [128, S], BF16) for i in range(2)]
        Ka = [sb("Ka%d" % i, [128, S], BF16) for i in range(2)]
        Va = [sb("Va%d" % i, [128, NKB, 65], BF16) for i in range(2)]
        Qs = sb("Qs", [128, S], BF16)
        Ks = sb("Ks", [128, S], BF16)
        Vs = sb("Vs", [128, NKB, 128], BF16)
        negtri = sb("negtri", [128, 128], BF16)
        negones = sb("negones", [128, 128], BF16)
        masks = sb("masks", [128, 8, 512], F32)
        onesf = sb("onesf", [128, 512], F32)
        tri32 = sb("tri32", [32, 32], F32)
        offs = sb("offs", [32, 2], F32)
        NPM, NTMF, NE1, NSP, NSS, NEG, NWT = 4, 2, 6, 5, 6, 3, 4
        pm = [sb("pm%d" % i, [128, 512], BF16) for i in range(NPM)]
        tmf = [sb("tmf%d" % i, [128, 512], F32) for i in range(NTMF)]
        e1 = [sb("e1_%d" % i, [128, 512], F32) for i in range(NE1)]
        sp = [sb("sp_%d" % i, [128, 512], BF16) for i in range(NSP)]
        ssum = [sb("ssum_%d" % i, [128, 512], BF16) for i in range(NSS)]
        eG = [sb("eG_%d" % i, [128, 512], F32) for i in range(NEG)]
        wt = [sb("wt_%d" % i, [128, 512], BF16) for i in range(NWT)]
        den = sb("den", [128, 512], F32)
        rden = sb("rden", [64, 512], F32)
        ost = [sb("ost%d" % i, [64, 512], BF16) for i in range(4)]
        ps = [stack.enter_context(nc.psum_tensor("ps%d" % i, [128, 512], F32)) for i in range(8)]

        P.dma("sp", negtri[:, :], cbf[:, 0:128], writes=["c_tri"])
        P.dma("sp", negones[:, :], cbf[:, 128:256], writes=["c_ones"])
        P.dma("sp", tri32[:, :], ctri[:, :], writes=["c_tri32"])
        P.memset("dve", onesf[:, :], 1.0, ["c_onesf"])
        dummy = sb("warm_rhs", [128, 512], BF16)
        P.memset("dve", dummy[:, :], 0.0, ["c_dummy"])

        vvf = vf.rearrange("(b p) d -> p b d", p=128)
        for hh in range(2):
            P.dma("sp", Qa[hh][0:64, :], qf[64 * hh:64 * hh + 64, :], writes=[("Qf", hh)])
            P.dma("sp", Ka[hh][0:64, :], kf[64 * hh:64 * hh + 64, :], writes=[("Kf", hh)])
            for half in range(2):
                P.dma("sp", Va[hh][:, 32 * half:32 * half + 32, 0:64], vvf[:, 32 * half:32 * half + 32, 64 * hh:64 * hh + 64],
                      writes=[("Vf", hh)], semkey=("Vf", hh))
            P.memset("dve", Va[hh][:, :, 64:65], 1.0, [("V1", hh)])
            P.memset("dve", Qa[hh][64:70, :], 1.0, [("Qx", hh)])
            P.memset("dve", Ka[hh][64:70, :], 1.0, [("Kx", hh)])
            if hh == 0:
                P.dma("sp", Qs[:, :], qs[:, :], writes=["Qs"])
                P.dma("sp", Ks[:, :], ks[:, :], writes=["Ks"])
                vvs = vs.rearrange("(b p) d -> p b d", p=128)
                for half in range(2):
                    P.dma("sp", Vs[:, 32 * half:32 * half + 32, :], vvs[:, 32 * half:32 * half + 32, :], writes=["Vs"],
                          semkey="Vs")
        P.dma("sp", masks[:, :, :], cf32.rearrange("p (a b) -> p a b", a=8), writes=["c_masks"])

        R = 2 * NFC
        lf32, lfk = e1[0][0:R, :], ("e1", 0)
        loc, lock = eG[0][0:R, :], ("eG", 0)
        Ft, Fk = tmf[0][0:R, :], ("tmf", 0)
        r1, r1k = tmf[1][0:R, :], ("tmf", 1)
        r2, r2k = e1[1][0:R, :], ("e1", 1)
        P.dma("sp", lf32, lf.rearrange("h (c n) -> (h c) n", n=FCH), writes=[lfk])
        P.op("dve", lambda e: e.tensor_tensor_scan(out=loc, data0=onesf[0:R, :], data1=lf32, initial=0.0,
                                                  op0=ALU.mult, op1=ALU.add), reads=[lfk, "c_onesf"], writes=[lock])
        P.mm(ps[4][0:R, 0:1], tri32[:, :], loc[:, FCH - 1:FCH], True, True, reads=[lock, "c_tri32"], writes=[("ps", 4)])
        P.act(offs[:, 0:1], ps[4][0:R, 0:1], AF.Copy, reads=[("ps", 4)], writes=["offs"])
        P.ts(Ft, loc, offs[:, 0:1], None, ALU.add, None, reads=[lock, "offs"], writes=[Fk])
        hi, hik = sp[0][0:R, :], ("sp", 0)
        mid, midk = sp[1][0:R, :], ("sp", 1)
        lo, lok = sp[2][0:R, :], ("sp", 2)
        P.copy("dve", hi, Ft, reads=[Fk], writes=[hik])
        P.tt(r1, Ft, hi, ALU.subtract, reads=[Fk, hik], writes=[r1k])
        P.copy("dve", mid, r1, reads=[r1k], writes=[midk])
        P.tt(r2, r1, mid, ALU.subtract, reads=[r1k, midk], writes=[r2k])
        P.copy("dve", lo, r2, reads=[r2k], writes=[lok])
        negs = []
        for j, (b, bk) in enumerate(((hi, hik), (mid, midk), (lo, lok))):
            nb_, nbk = wt[j][0:R, :], ("wt", j)
            P.ts(nb_, b, -1.0, None, ALU.mult, None, reads=[bk], writes=[nbk], eng="pool")
            negs.append((nb_, nbk))
        for hh in range(2):
            for j, ((b, bk), (nb_, nbk)) in enumerate(zip(((hi, hik), (mid, midk), (lo, lok)), negs)):
                for ch in range(NFC):
                    r = NFC * hh + ch
                    P.dma("sp", Qa[hh][64 + j:65 + j, ch * FCH:(ch + 1) * FCH], b[r:r + 1, :], reads=[bk, ("Qx", hh)],
                          writes=[("Qg", hh)], semkey=("Qg", hh))
                    P.dma("sp", Ka[hh][67 + j:68 + j, ch * FCH:(ch + 1) * FCH], nb_[r:r + 1, :], reads=[nbk, ("Kx", hh)],
                          writes=[("Kg", hh)], semkey=("Kg", hh))

        cnt = {"pm": 0, "tmf": 0, "e1": 0, "sp": 0, "ss": 0, "eG": 0, "wt": 0, "Ss": 0, "Sf": 0, "G": 0, "ost": 0}

        def nxt(name, n):
            i = cnt[name]
            cnt[name] = i + 1
            return i % n

        for pair in range(2):
            fq = [("Qf", pair), ("Qg", pair), ("Qx", pair)]
            fk = [("Kf", pair), ("Kg", pair), ("Kx", pair)]
            fv = [("Vf", pair), ("V1", pair)]
            r0 = 64 * pair
            sbq = Qs[r0:r0 + 64, :]
            sbk = Ks[r0:r0 + 64, :]
            fblocks = [(qt, kb) for qt in range(NQT) for kb in range(4 * (qt + 1))]
            sblocks = [(qt, kb) for qt in range(NQT) for kb in range(4 * (qt + 1) - 1, -1, -1)]
            NBL = len(fblocks)
            fst = {}
            sst = {}
            pending = []
            lastG = [4]

            def fox_qk(g):
                qt, kb = fblocks[g]
                si = 2 + nxt("Sf", 2)
                fst[g] = dict(S=si)
                P.mm(ps[si][:, :], Ka[pair][0:70, kb * 128:(kb + 1) * 128], Qa[pair][0:70, qt * QT_W:(qt + 1) * QT_W],
                     True, True, reads=fq + fk, writes=[("ps", si)])

            def fox_exp(g):
                qt, kb = fblocks[g]
                si = fst[g]["S"]
                j = kb - 4 * qt
                pi = nxt("pm", NPM)
                fst[g]["pm"] = pi
                if j >= 0:
                    ti = nxt("tmf", NTMF)
                    P.tt(tmf[ti][:, :], ps[si][:, :], masks[:, j, :], ALU.add, reads=[("ps", si), "c_masks"],
                         writes=[("tmf", ti)])
                    P.act(pm[pi][:, :], tmf[ti][:, :], AF.Exp, reads=[("tmf", ti)], writes=[("pm", pi)])
                else:
                    P.act(pm[pi][:, :], ps[si][:, :], AF.Exp, reads=[("ps", si)], writes=[("pm", pi)])

            def fox_pv(g):
                qt, kb = fblocks[g]
                nkb = 4 * (qt + 1)
                pi = fst.pop(g)["pm"]
                P.mm(ps[6][0:65, :], Va[pair][:, kb, 0:65], pm[pi][:, :], kb == 0, kb == nkb - 1,
                     reads=fv + [("pm", pi)], writes=[("ps", 6)])
                if kb == nkb - 1:
                    pending.append(qt)

            def fox_epilogue(qt):
                q0 = qt * QT_W
                gi = lastG[0]
                P.act(den[64:65, :], ps[6][64:65, :], AF.Copy, reads=[("ps", 6)], writes=["den"])
                P.mm(ps[gi][0:64, :], onesf[64:65, 0:64], den[64:65, :], True, True, reads=["den", "c_onesf"],
                     writes=[("ps", gi)])
                P.recip(rden[:, :], ps[gi][0:64, :], reads=[("ps", gi)], writes=["rden"])
                oi = nxt("ost", 4)
                P.tt(ost[oi][:, :], ps[6][0:64, :], rden[:, :], ALU.mult, reads=[("ps", 6), "rden"],
                     writes=[("ost", oi)])
                P.dma("sp", oT[r0:r0 + 64, q0:q0 + QT_W], ost[oi][:, :], reads=[("ost", oi)])

            def sb_qk(g):
                qt, kb = sblocks[g]
                si = nxt("Ss", 2)
                sst.setdefault(g, {})["S"] = si
                P.mm(ps[si][:, :], sbk[:, kb * 128:(kb + 1) * 128], sbq[:, qt * QT_W:(qt + 1) * QT_W], True, True,
                     reads=["Qs", "Ks"], writes=[("ps", si)])

            def sb_exp(g):
                qt, kb = sblocks[g]
                d = sst[g]
                si = d["S"]
                ei = nxt("e1", NE1)
                d["e1"] = ei
                P.act(e1[ei][:, :], ps[si][:, :], AF.Exp, reads=[("ps", si)], writes=[("e1", ei)])
                j = kb - 4 * qt
                if j >= 0:
                    P.tt(e1[ei][:, :], e1[ei][:, :], masks[:, 4 + j, :], ALU.mult, reads=[("e1", ei), "c_masks"],
                         writes=[("e1", ei)], eng="pool")

            def sb_ln(g):
                qt, kb = sblocks[g]
                nkb = 4 * (qt + 1)
                k = nkb - 1 - kb
                d = sst[g]
                ei = d["e1"]
                spi = nxt("sp", NSP)
                d["sp"] = spi
                P.act(sp[spi][:, :], e1[ei][:, :], AF.Ln, reads=[("e1", ei), "c_onesf"], writes=[("sp", spi)],
                      bias=onesf[:, 0:1], scale=1.0)
                if k + 1 < nkb:
                    nsi = nxt("ss", NSS)
                    sst.setdefault(g + 1, {})["ss"] = nsi
                    if k == 0:
                        P.copy("pool", ssum[nsi][:, :], sp[spi][:, :], reads=[("sp", spi)], writes=[("ss", nsi)])
                    else:
                        ci = d["ss"]
                        P.tt(ssum[nsi][:, :], ssum[ci][:, :], sp[spi][:, :], ALU.add,
                             reads=[("ss", ci), ("sp", spi)], writes=[("ss", nsi)], eng="pool")

            def sb_s2a(g):
                qt, kb = sblocks[g]
                nkb = 4 * (qt + 1)
                k = nkb - 1 - kb
                d = sst[g]
                gi = 4 + nxt("G", 2)
                d["G"] = gi
                spi = d["sp"]
                P.mm(ps[gi][:, :], negtri[:, :], sp[spi][:, :], True, k == 0, reads=[("sp", spi), "c_tri"],
                     writes=[("ps", gi)])
                if k > 0:
                    ci = d["ss"]
                    P.mm(ps[gi][:, :], negones[:, :], ssum[ci][:, :], False, True, reads=[("ss", ci), "c_ones"],
                         writes=[("ps", gi)])

            def sb_s2b(g):
                d = sst[g]
                gi = d["G"]
                lastG[0] = gi
                gi2 = nxt("eG", NEG)
                P.act(eG[gi2][:, :], ps[gi][:, :], AF.Exp, reads=[("ps", gi)], writes=[("eG", gi2)])
                wi = nxt("wt", NWT)
                d["wt"] = wi
                P.tt(wt[wi][:, :], e1[d["e1"]][:, :], eG[gi2][:, :], ALU.mult, reads=[("e1", d["e1"]), ("eG", gi2)],
                     writes=[("wt", wi)])

            def sb_s3(g):
                qt, kb = sblocks[g]
                nkb = 4 * (qt + 1)
                k = nkb - 1 - kb
                d = sst.pop(g)
                wi = d["wt"]
                P.mm(ps[7][0:64, :], Vs[:, kb, r0:r0 + 64], wt[wi][:, :], k == 0, k == nkb - 1,
                     reads=["Vs", ("wt", wi)], writes=[("ps", 7)])
                if k == nkb - 1:
                    q0 = qt * QT_W
                    oi = nxt("ost", 4)
                    P.act(ost[oi][:, :], ps[7][0:64, :], AF.Copy, reads=[("ps", 7)], writes=[("ost", oi)])
                    P.dma("sp", oT[128 + r0:128 + r0 + 64, q0:q0 + QT_W], ost[oi][:, :], reads=[("ost", oi)])

            ok = lambda x: 0 <= x < NBL
            for g in range(-1, NBL + 5):
                if ok(g + 1):
                    sb_qk(g + 1)
                if ok(g):
                    fox_qk(g)
                if ok(g - 2):
                    fox_pv(g - 2)
                    sb_s2a(g - 2)
                if ok(g - 4):
                    sb_s3(g - 4)
                for _ in range(ND_WARM):
                    P.mm(ps[7][64:128, :], negones[:, 0:64], dummy[:, :], True, True, reads=["c_ones", "c_dummy"], writes=())
                if ok(g):
                    sb_exp(g)
                if ok(g - 1):
                    fox_exp(g - 1)
                if ok(g):
                    sb_ln(g)
                if ok(g - 3):
                    sb_s2b(g - 3)
                while pending:
                    fox_epilogue(pending.pop(0))
        P.emit()
    return nc


def _bf(a):
    return np.ascontiguousarray(a).astype(ml_dtypes.bfloat16)


def _cols(vec, nchunk):
    return np.ascontiguousarray(np.asarray(vec, np.float32).reshape(nchunk, 128).T)


def _taps(w, nchunk):
    return np.ascontiguousarray(np.asarray(w, np.float32).T.reshape(nchunk, 128, 3).transpose(1, 0, 2))


def _t_consts():
    ones = np.ones((128, 128), np.float32)
    bd = np.zeros((128, 128), np.float32)
    bd[:64, :64] = 1.0
    bd[64:, 64:] = 1.0
    return _bf(np.concatenate([ones, bd], axis=1))


def _a_consts():
    j = np.arange(128)[:, None]
    s = np.arange(128)[None, :]
    negtri = np.where(j >= s, -1.0, 0.0).astype(np.float32)
    negones = -np.ones((128, 128), np.float32)
    cbf = _bf(np.concatenate([negtri, negones], axis=1))
    sl = np.arange(128)[:, None]
    tl = np.arange(512)[None, :]
    madd = np.stack([np.where(128 * jj + sl <= tl, 0.0, -30000.0) for jj in range(4)], axis=1)
    m01 = np.stack([np.where(128 * jj + sl < tl, 1.0, 0.0) for jj in range(4)], axis=1)
    cf = np.concatenate([madd, m01], axis=1).astype(np.float32).reshape(128, 8 * 512)
    pp = np.arange(32)
    tri = ((pp[:, None] < pp[None, :]) & (pp[:, None] // 16 == pp[None, :] // 16)).astype(np.float32)
    return cbf, (np.ascontiguousarray(cf), np.ascontiguousarray(tri))


def _pad_cols(full, j, dtype):
    out = np.zeros((full.shape[0], NCOL), dtype)
    t0 = NTOK * j
    out[:, PAD:] = full[:, t0:t0 + NTOK]
    if j > 0:
        out[:, 2:PAD] = full[:, t0 - 6:t0]
    return out


def _attn_inputs(inp, i):
    gq = np.concatenate([np.tile(inp["fox_q_gain"][i], 8), np.tile(inp["sb_q_gain"][i], 8)])
    gk = np.concatenate([np.tile(inp["fox_k_gain"][i], 8), np.tile(inp["sb_k_gain"][i], 8)])
    return {
        "at_g": _cols(inp["attn_norm"][i], 8),
        "at_in": np.ascontiguousarray(inp["attn_w_in"][i], dtype=np.float32),
        "at_gqk": np.ascontiguousarray(np.concatenate([_cols(gq, 8), _cols(gk, 8)], axis=1)),
        "at_fb": np.ascontiguousarray(np.asarray(inp["attn_f_bias"][i], np.float32).reshape(8, 1)),
    }


def _block_inputs(inp, i):
    d = {"w_o": np.ascontiguousarray(inp["attn_w_out"][i], dtype=np.float32)}
    for jj, l in enumerate((2 * i, 2 * i + 1)):
        d["ffn_g%d" % jj] = _cols(inp["ffn_norm"][l], 8)
        d["ffn_up%d" % jj] = np.ascontiguousarray(inp["ffn_w_up"][l], dtype=np.float32)
        d["ffn_cw%d" % jj] = _taps(inp["ffn_conv"][l], 44)
        d["ffn_dn%d" % jj] = np.ascontiguousarray(inp["ffn_w_down"][l], dtype=np.float32)
    d["cv_g"] = _cols(inp["conv_norm"][i], 8)
    d["cv_in"] = np.ascontiguousarray(inp["conv_w_in"][i], dtype=np.float32)
    d["cv_k"] = _taps(inp["conv_kernel"][i], 8)
    d["cv_out"] = np.ascontiguousarray(inp["conv_w_out"][i], dtype=np.float32)
    return d


def _run(nc, in_maps):
    res = run_bass_kernel_spmd(nc, in_maps, core_ids=list(range(8)))
    return res.results


def _gather_fm(results, name):
    return [np.concatenate([np.asarray(results[4 * b + j][name]) for j in range(4)], axis=1) for b in range(NB)]


def _attention(results, cbf, cf):
    QT = _gather_fm(results, "QT")
    KT = _gather_fm(results, "KT")
    LF = _gather_fm(results, "LF")
    V = [np.concatenate([np.asarray(results[4 * b + j]["V"]) for j in range(4)], axis=0) for b in range(NB)]
    maps = []
    for c in range(8):
        b, g = divmod(c, 4)
        f0, s0 = 128 * g, 512 + 128 * g
        maps.append({
            "qf": np.ascontiguousarray(QT[b][f0:f0 + 128]), "kf": np.ascontiguousarray(KT[b][f0:f0 + 128]),
            "vf": np.ascontiguousarray(V[b][:, f0:f0 + 128]), "lf": np.ascontiguousarray(LF[b][2 * g:2 * g + 2]),
            "qs": np.ascontiguousarray(QT[b][s0:s0 + 128]), "ks": np.ascontiguousarray(KT[b][s0:s0 + 128]),
            "vs": np.ascontiguousarray(V[b][:, s0:s0 + 128]), "acst_bf": cbf, "acst_f32": cf[0], "acst_tri": cf[1],
        })
    ares = _run(build_A(), maps)
    oT = []
    for b in range(NB):
        o = np.zeros((D, S), ml_dtypes.bfloat16)
        for g in range(4):
            r = np.asarray(ares[4 * b + g]["oT"])
            o[128 * g:128 * g + 128] = r[0:128]
            o[512 + 128 * g:512 + 128 * g + 128] = r[128:256]
        oT.append(o)
    return oT


def kernel(**inp):
    inp = {k: np.asarray(v) for k, v in inp.items()}
    x = inp["x"].astype(np.float32, copy=False)
    tc = _t_consts()
    cbf, cf = _a_consts()
    hfull = [np.ascontiguousarray(x[b].T) for b in range(NB)]
    maps = []
    for c in range(8):
        b, j = divmod(c, 4)
        m = {"hin": _pad_cols(hfull[b], j, np.float32), "cst_bf": tc}
        m.update(_attn_inputs(inp, 0))
        maps.append(m)
    r1 = _run(build_T(1), maps)
    oT = _attention(r1, cbf, cf)
    maps = []
    for c in range(8):
        b, j = divmod(c, 4)
        m = {"hin": _pad_cols(hfull[b], j, np.float32), "oT": _pad_cols(oT[b], j, ml_dtypes.bfloat16), "cst_bf": tc}
        m.update(_block_inputs(inp, 0))
        m.update(_attn_inputs(inp, 1))
        maps.append(m)
    r2 = _run(build_T(2), maps)
    hfull = _gather_fm(r2, "hout")
    oT = _attention(r2, cbf, cf)
    maps = []
    for c in range(8):
        b, j = divmod(c, 4)
        m = {"hin": _pad_cols(hfull[b], j, np.float32), "oT": _pad_cols(oT[b], j, ml_dtypes.bfloat16), "cst_bf": tc}
        m.update(_block_inputs(inp, 1))
        maps.append(m)
    r3 = _run(build_T(3), maps)
    hfull = _gather_fm(r3, "hout")
    out = np.stack([np.ascontiguousarray(hfull[b].T) for b in range(NB)], axis=0)
    return out.astype(np.float32)
```

```python
import numpy as np
import ml_dtypes
from contextlib import ExitStack
import concourse.bass as bass
import concourse.mybir as mybir
from concourse.bass_utils import run_bass_kernel_spmd

F32 = mybir.dt.float32
BF16 = mybir.dt.bfloat16
AF = mybir.ActivationFunctionType
ALU = mybir.AluOpType

D = 1024
S = 8192
NB = 2
DFF = 2816
EPS = 1e-6
NTOK = 2048
PAD = 8
NCOL = NTOK + PAD
TILES = [(2, 346), (346, 690), (690, 1032), (1032, 1374), (1374, 1716), (1716, 2056)]
GROUPS = [(0, 1, 2), (3, 4, 5)]
GBASE = [2, 1032]
GW = 1032
A_LOFF = [0, 344, 688]
TW = 346
NPS = 8


class Prog:
    ENGS = ("sp", "pe", "act", "dve", "pool")

    def __init__(self, nc, stack):
        self.nc = nc
        self.stack = stack
        self.streams = {e: [] for e in self.ENGS}
        self.esem = {}
        self.ecnt = {}
        self.sid = 0
        for e in ("pe", "act", "dve", "pool"):
            self.esem[e] = (self._sid(), stack.enter_context(nc.semaphore("s_" + e)))
            self.ecnt[e] = 0
        self.waited = {e: {} for e in self.ENGS}
        self.res = {}
        self.dsem = {}
        self.ps_i = 0
        self.tmp_i = 0

    def _sid(self):
        self.sid += 1
        return self.sid

    def _deps(self, eng, reads, writes):
        need = {}
        wd = self.waited[eng]

        def add(ev):
            if ev is None:
                return
            sid, sem, val, src = ev
            if eng == "pe" and src == "pe":
                return
            if wd.get(sid, 0) >= val:
                return
            if sid not in need or need[sid][1] < val:
                need[sid] = (sem, val)

        for r in reads:
            st = self.res.get(r)
            if st is not None:
                add(st[0])
        for w in writes:
            st = self.res.get(w)
            if st is not None:
                add(st[0])
                for ev in st[1].values():
                    add(ev)
        for sid, (sem, val) in need.items():
            wd[sid] = val
        return list(need.values())

    def _record(self, ev, reads, writes):
        for w in writes:
            self.res[w] = [ev, {}]
        ws = set(writes)
        for r in reads:
            if r in ws:
                continue
            st = self.res.get(r)
            if st is None:
                st = [None, {}]
                self.res[r] = st
            st[1][ev[0]] = ev

    def op(self, eng, fn, reads=(), writes=()):
        waits = self._deps(eng, reads, writes)
        self.ecnt[eng] += 1
        sid, sem = self.esem[eng]
        ev = (sid, sem, self.ecnt[eng], eng)
        self.streams[eng].append((waits, fn, (sem, 1)))
        self._record(ev, reads, writes)

    def dma(self, q, out, in_, reads=(), writes=(), semkey=None):
        waits = self._deps(q, reads, writes)
        key = semkey if semkey is not None else (writes[0] if writes else ("st", reads[0]))
        ent = self.dsem.get(key)
        if ent is None:
            ent = [self._sid(), self.stack.enter_context(self.nc.semaphore("d%d" % len(self.dsem))), 0]
            self.dsem[key] = ent
        ent[2] += 16
        ev = (ent[0], ent[1], ent[2], "dma")
        self.streams[q].append((waits, lambda e: e.dma_start(out=out, in_=in_), (ent[1], 16)))
        self._record(ev, reads, writes)

    def mm(self, out, lhsT, rhs, start, stop, reads, writes):
        self.op("pe", lambda e: e.matmul(out, lhsT=lhsT, rhs=rhs, start=start, stop=stop), reads, writes)

    def act(self, out, in_, func, reads, writes, **kw):
        self.op("act", lambda e: e.activation(out=out, in_=in_, func=func, **kw), reads, writes)

    def stt(self, out, in0, scalar, in1, op0, op1, reads, writes, eng="dve"):
        self.op(eng, lambda e: e.scalar_tensor_tensor(out=out, in0=in0, scalar=scalar, in1=in1, op0=op0, op1=op1),
                reads, writes)

    def tt(self, out, in0, in1, op, reads, writes, eng="dve"):
        self.op(eng, lambda e: e.tensor_tensor(out=out, in0=in0, in1=in1, op=op), reads, writes)

    def ts(self, out, in0, s1, s2, op0, op1, reads, writes, eng="dve"):
        if op1 is None:
            self.op(eng, lambda e: e.tensor_scalar(out=out, in0=in0, scalar1=s1, scalar2=None, op0=op0), reads, writes)
        else:
            self.op(eng, lambda e: e.tensor_scalar(out=out, in0=in0, scalar1=s1, scalar2=s2, op0=op0, op1=op1),
                    reads, writes)

    def recip(self, out, in_, reads, writes):
        self.op("dve", lambda e: e.reciprocal(out=out, in_=in_), reads, writes)

    def copy(self, eng, out, in_, reads, writes):
        self.op(eng, lambda e: e.tensor_copy(out=out, in_=in_), reads, writes)

    def memset(self, eng, ap, val, writes):
        self.op(eng, lambda e: e.memset(ap, val), (), writes)

    def emit(self):
        nc = self.nc
        fin = [(ent[1], ent[2]) for ent in self.dsem.values()]
        with nc.Block() as block:
            for eng, reg in (("sp", block.sync), ("pe", block.tensor), ("act", block.scalar),
                             ("dve", block.vector), ("pool", block.gpsimd)):
                items = self.streams[eng]

                def body(e, items=items, eng=eng):
                    for waits, fn, inc in items:
                        for sem, val in waits:
                            e.wait_ge(sem, val)
                        fn(e).then_inc(inc[0], inc[1])
                    if eng == "sp":
                        for sem, val in fin:
                            e.wait_ge(sem, val)

                reg(body)


class TCtx:
    pass


def _wview(W, c0, w):
    return W.rearrange("(k p) c -> p k c", p=128)[:, :, c0:c0 + w]


def t_alloc(nc, stack, P):
    C = TCtx()
    C.P = P
    sb = lambda name, shape, dt: stack.enter_context(nc.sbuf_tensor(name, shape, dt))
    C.hT = sb("hT", [128, 8, NCOL], F32)
    C.xn = sb("xn", [128, 8, NCOL], BF16)
    C.a = sb("abuf", [128, 22, GW], BF16)
    C.wbuf = [sb("wbuf%d" % i, [128, 6144], BF16) for i in range(2)]
    C.sq = [sb("sq%d" % i, [128, 8, TW], BF16) for i in range(2)]
    C.tmp = [sb("tmp%d" % i, [128, TW], F32) for i in range(8)]
    C.stg = [sb("stg%d" % i, [128, 512], BF16) for i in range(6)]
    C.lf = [sb("lf%d" % i, [8, TW], F32) for i in range(6)]
    C.ones = sb("ones_bf", [128, 128], BF16)
    C.bd = sb("bd_bf", [128, 128], BF16)
    C.ps = [stack.enter_context(nc.psum_tensor("ps%d" % i, [128, 512], F32)) for i in range(NPS)]
    C.wi = 0
    C.sqi = 0
    C.tmpi = 0
    C.psi = 0
    C.stgi = 0
    C.lfi = 0
    return C


def _ps(C):
    i = C.psi
    C.psi = (i + 1) % NPS
    return C.ps[i], ("ps", i)


def _tmp(C):
    i = C.tmpi
    C.tmpi = (i + 1) % len(C.tmp)
    return C.tmp[i], ("tmp", i)


def _stg(C):
    i = C.stgi
    C.stgi = (i + 1) % len(C.stg)
    return C.stg[i], ("stg", i)


def _lf(C):
    i = C.lfi
    C.lfi = (i + 1) % len(C.lf)
    return C.lf[i], ("lf", i)


def _loadw(C, parts, kdim):
    P = C.P
    i = C.wi
    C.wi = (i + 1) % 2
    wt = C.wbuf[i]
    tot = sum(w for _, _, w in parts)
    view = wt[:, 0:kdim * tot].rearrange("p (k c) -> p k c", k=kdim)
    off = 0
    for W, c0, w in parts:
        P.dma("pool", view[:, :, off:off + w], _wview(W, c0, w), reads=(), writes=[("w", i)], semkey=("w", i))
        off += w
    return view, ("w", i)


def t_norm(C, gcol, tiles=range(6)):
    P = C.P
    for ti in tiles:
        s, e = TILES[ti]
        s0 = 0 if ti == 0 else s
        n = e - s0
        sq = C.sq[C.sqi]
        sqk = ("sq", C.sqi)
        C.sqi = (C.sqi + 1) % 2
        hk = [("h", c, ti) for c in range(8)]
        P.act(sq[:, :, 0:n], C.hT[:, :, s0:e], AF.Square, reads=hk, writes=[sqk])
        ps, psk = _ps(C)
        for c in range(8):
            P.mm(ps[:, 0:n], C.ones[:, :], sq[:, c, 0:n], c == 0, c == 7, reads=[sqk, "const"], writes=[psk])
        rt, rtk = _tmp(C)
        P.act(rt[:, 0:n], ps[:, 0:n], AF.Sqrt, reads=[psk], writes=[rtk], scale=1.0 / D, bias=C.eps1[:, 0:1])
        P.recip(rt[:, 0:n], rt[:, 0:n], reads=[rtk], writes=[rtk])
        for c in range(8):
            P.stt(C.xn[:, c, s0:e], C.hT[:, c, s0:e], gcol[:, c:c + 1], rt[:, 0:n], ALU.mult, ALU.mult,
                  reads=[("h", c, ti), rtk, "const"], writes=[("xn", c, ti)])


def _xnk(ti, halo=True):
    ks = [("xn", c, ti) for c in range(8)]
    if halo and ti > 0:
        ks += [("xn", c, ti - 1) for c in range(8)]
    return ks


def _conv3(C, ps, psk, n, wcol, j):
    P = C.P
    t, tk = _tmp(C)
    P.act(t[:, 0:n], ps[:, 2:n + 2], AF.Identity, reads=[psk, "const"], writes=[tk], scale=wcol[:, j, 2:3])
    P.stt(t[:, 0:n], ps[:, 1:n + 1], wcol[:, j, 1:2], t[:, 0:n], ALU.mult, ALU.add, reads=[psk, tk, "const"], writes=[tk])
    P.stt(t[:, 0:n], ps[:, 0:n], wcol[:, j, 0:1], t[:, 0:n], ALU.mult, ALU.add, reads=[psk, tk, "const"], writes=[tk])
    return t, tk


def t_proj_down(C, W, kdim, gi):
    P = C.P
    for ob in range(4):
        wt, wk = _loadw(C, [(W, 256 * ob, 256)], kdim)
        for ti in GROUPS[gi]:
            s, e = TILES[ti]
            n = e - s
            ls = A_LOFF[ti % 3]
            for jj in range(2):
                jc = 2 * ob + jj
                ps, psk = _ps(C)
                for k in range(kdim):
                    P.mm(ps[:, 0:n], wt[:, k, jj * 128:(jj + 1) * 128], C.a[:, k, ls:ls + n], k == 0, k == kdim - 1,
                         reads=[wk, ("a", k, ti % 3)], writes=[psk])
                P.tt(C.hT[:, jc, s:e], ps[:, 0:n], C.hT[:, jc, s:e], ALU.add, reads=[psk, ("h", jc, ti)],
                     writes=[("h", jc, ti)])


def t_outproj(C, oT, W):
    P = C.P
    for gi in range(2):
        for ti in GROUPS[gi]:
            s, e = TILES[ti]
            ls = A_LOFF[ti % 3]
            P.dma("sp", C.a[:, 0:8, ls:ls + (e - s)], oT.rearrange("(k p) c -> p k c", p=128)[:, :, s:e],
                  reads=(), writes=[("a", k, ti % 3) for k in range(8)], semkey=("aload", ti))
        t_proj_down(C, W, 8, gi)


def t_ffn(C, gcol, W_up, W_dn, cw):
    P = C.P
    t_norm(C, gcol)
    for gi in range(2):
        for ub in range(11):
            wt, wk = _loadw(C, [(W_up, 256 * ub, 256), (W_up, DFF + 256 * ub, 256)], 8)
            for ti in GROUPS[gi]:
                s, e = TILES[ti]
                n = e - s
                ls = A_LOFF[ti % 3]
                xk = _xnk(ti)
                for pj in range(2):
                    i = 2 * ub + pj
                    gp, gk = _ps(C)
                    vp, vk = _ps(C)
                    for kc in range(8):
                        P.mm(gp[:, 0:n + 2], wt[:, kc, pj * 128:(pj + 1) * 128], C.xn[:, kc, s - 2:e], kc == 0, kc == 7,
                             reads=[wk] + xk, writes=[gk])
                    for kc in range(8):
                        P.mm(vp[:, 0:n + 2], wt[:, kc, 256 + pj * 128:256 + (pj + 1) * 128], C.xn[:, kc, s - 2:e],
                             kc == 0, kc == 7, reads=[wk] + xk, writes=[vk])
                    t1, t1k = _conv3(C, gp, gk, n, cw, i)
                    t2, t2k = _conv3(C, vp, vk, n, cw, 22 + i)
                    P.act(t1[:, 0:n], t1[:, 0:n], AF.Silu, reads=[t1k], writes=[t1k])
                    P.tt(C.a[:, i, ls:ls + n], t1[:, 0:n], t2[:, 0:n], ALU.mult, reads=[t1k, t2k],
                         writes=[("a", i, ti % 3)])
        t_proj_down(C, W_dn, 22, gi)


def t_convmix(C, gcol, W_in, ck, W_out):
    P = C.P
    t_norm(C, gcol)
    for gi in range(2):
        for cb in range(4):
            wt, wk = _loadw(C, [(W_in, 256 * cb, 256), (W_in, 1024 + 256 * cb, 256), (W_in, 2048 + 256 * cb, 256)], 8)
            for ti in GROUPS[gi]:
                s, e = TILES[ti]
                n = e - s
                ls = A_LOFF[ti % 3]
                xk = _xnk(ti)
                for cj in range(2):
                    i = 2 * cb + cj
                    pss = []
                    for part in range(3):
                        pp, pk = _ps(C)
                        for kc in range(8):
                            P.mm(pp[:, 0:n + 2], wt[:, kc, 256 * part + cj * 128:256 * part + (cj + 1) * 128],
                                 C.xn[:, kc, s - 2:e], kc == 0, kc == 7, reads=[wk] + xk, writes=[pk])
                        pss.append((pp, pk))
                    (bp, bk), (cp, ck_), (up, uk) = pss
                    tc_, tck = _tmp(C)
                    P.act(tc_[:, 0:n + 2], cp[:, 0:n + 2], AF.Copy, reads=[ck_], writes=[tck])
                    P.tt(tc_[:, 0:n + 2], up[:, 0:n + 2], tc_[:, 0:n + 2], ALU.mult, reads=[uk, tck], writes=[tck])
                    t3, t3k = _conv3(C, tc_, tck, n, ck, i)
                    P.tt(C.a[:, i, ls:ls + n], bp[:, 2:n + 2], t3[:, 0:n], ALU.mult, reads=[bk, t3k],
                         writes=[("a", i, ti % 3)])
        t_proj_down(C, W_out, 8, gi)


def t_qkv(C, gcol, W_in, gqk, fb, wf, QT, KT, V, LF):
    P = C.P
    t_norm(C, gcol)
    for gi in range(2):
        for qb in range(8):
            wt, wk = _loadw(C, [(W_in, 256 * qb, 256)], 8)
            for ti in GROUPS[gi]:
                s, e = TILES[ti]
                n = e - s
                xk = _xnk(ti, halo=False)
                so = max(s, PAD)
                for cj in range(2):
                    c = 2 * qb + cj
                    isq = c < 8
                    pp, pk = _ps(C)
                    for kc in range(8):
                        P.mm(pp[:, 0:n], wt[:, kc, cj * 128:(cj + 1) * 128], C.xn[:, kc, s:e], kc == 0, kc == 7,
                             reads=[wk] + xk, writes=[pk])
                    sq, sqk = _stg(C)
                    P.act(sq[:, 0:n], pp[:, 0:n], AF.Square, reads=[pk], writes=[sqk])
                    p2, p2k = _ps(C)
                    P.mm(p2[:, 0:n], C.bd[:, :], sq[:, 0:n], True, True, reads=[sqk, "const"], writes=[p2k])
                    rt, rtk = _tmp(C)
                    if isq:
                        P.act(rt[:, 0:n], p2[:, 0:n], AF.Sqrt, reads=[p2k], writes=[rtk], scale=1.0, bias=C.eps64[:, 0:1])
                    else:
                        P.act(rt[:, 0:n], p2[:, 0:n], AF.Sqrt, reads=[p2k], writes=[rtk], scale=1.0 / 64, bias=C.eps1[:, 0:1])
                    P.recip(rt[:, 0:n], rt[:, 0:n], reads=[rtk], writes=[rtk])
                    st, stk = _stg(C)
                    P.stt(st[:, 0:n], pp[:, 0:n], gqk[:, c:c + 1], rt[:, 0:n], ALU.mult, ALU.mult,
                          reads=[pk, rtk, "const"], writes=[stk])
                    dst = QT if isq else KT
                    cc = c % 8
                    P.dma("sp", dst[128 * cc:128 * (cc + 1), so - PAD:e - PAD], st[:, so - s:n], reads=[stk], writes=())
        for vb in range(2):
            wt, wk = _loadw(C, [(W_in, 2048 + 512 * vb, 512)], 8)
            for m in range(8):
                c0 = PAD + 1024 * gi + 128 * m
                tis = sorted({ti for ti in range(6) if TILES[ti][0] < c0 + 128 and TILES[ti][1] > c0})
                xk = [("xn", c, ti) for c in range(8) for ti in tis]
                pp, pk = _ps(C)
                for kc in range(8):
                    P.mm(pp[:, 0:512], C.xn[:, kc, c0:c0 + 128], wt[:, kc, 0:512], kc == 0, kc == 7, reads=[wk] + xk,
                         writes=[pk])
                st, stk = _stg(C)
                P.act(st[:, 0:512], pp[:, 0:512], AF.Copy, reads=[pk], writes=[stk])
                r0 = 1024 * gi + 128 * m
                P.dma("sp", V[r0:r0 + 128, 512 * vb:512 * (vb + 1)], st[:, 0:512], reads=[stk], writes=())
        for ti in GROUPS[gi]:
            s, e = TILES[ti]
            n = e - s
            so = max(s, PAD)
            xk = _xnk(ti, halo=False)
            pp, pk = _ps(C)
            for kc in range(8):
                P.mm(pp[0:8, 0:n], wf[:, kc, :], C.xn[:, kc, s:e], kc == 0, kc == 7, reads=["const"] + xk, writes=[pk])
            y, yk = _lf(C)
            P.act(y[:, 0:n], pp[0:8, 0:n], AF.Identity, reads=[pk, "const"], writes=[yk], bias=fb[:, 0:1], scale=1.0)
            ay, ayk = _lf(C)
            P.act(ay[:, 0:n], y[:, 0:n], AF.Abs, reads=[yk], writes=[ayk])
            P.act(ay[:, 0:n], ay[:, 0:n], AF.Exp, reads=[ayk], writes=[ayk], scale=-1.0)
            P.act(ay[:, 0:n], ay[:, 0:n], AF.Ln, reads=[ayk], writes=[ayk], bias=C.one1[0:8, 0:1], scale=1.0)
            P.ts(y[:, 0:n], y[:, 0:n], 0.0, None, ALU.min, None, reads=[yk], writes=[yk])
            P.tt(y[:, 0:n], y[:, 0:n], ay[:, 0:n], ALU.subtract, reads=[yk, ayk], writes=[yk])
            P.dma("sp", LF[0:8, so - PAD:e - PAD], y[:, so - s:n], reads=[yk], writes=())


def build_T(mode, upto=9):
    nc = bass.Bass("TRN2", target_bir_lowering=False)
    din = lambda name, shape, dt=F32: nc.dram_tensor(name, shape, dt, kind="ExternalInput").ap()
    dout = lambda name, shape, dt=F32: nc.dram_tensor(name, shape, dt, kind="ExternalOutput").ap()
    hin = din("hin", [D, NCOL])
    consts = din("cst_bf", [128, 256], BF16)
    if mode in (2, 3):
        oT = din("oT", [D, NCOL], BF16)
        w_o = din("w_o", [D, D])
        ffn = []
        for j in range(2):
            ffn.append(dict(g=din("ffn_g%d" % j, [128, 8]), up=din("ffn_up%d" % j, [D, 2 * DFF]),
                            cw=din("ffn_cw%d" % j, [128, 44, 3]), dn=din("ffn_dn%d" % j, [DFF, D])))
        cv = dict(g=din("cv_g", [128, 8]), w_in=din("cv_in", [D, 3 * D]), ck=din("cv_k", [128, 8, 3]),
                  w_out=din("cv_out", [D, D]))
        hout = dout("hout", [D, NTOK])
    if mode in (1, 2):
        at = dict(g=din("at_g", [128, 8]), w_in=din("at_in", [D, 3080]), gqk=din("at_gqk", [128, 16]),
                  fb=din("at_fb", [8, 1]))
        QT = dout("QT", [D, NTOK], BF16)
        KT = dout("KT", [D, NTOK], BF16)
        V = dout("V", [NTOK, D], BF16)
        LF = dout("LF", [8, NTOK])

    with ExitStack() as stack:
        P = Prog(nc, stack)
        C = t_alloc(nc, stack, P)
        sb = lambda name, shape, dt: stack.enter_context(nc.sbuf_tensor(name, shape, dt))
        C.eps1 = sb("eps1", [128, 1], F32)
        C.eps64 = sb("eps64", [128, 1], F32)
        C.one1 = sb("one1", [128, 1], F32)
        P.memset("dve", C.eps1[:, :], EPS, ["const"])
        P.memset("dve", C.eps64[:, :], 64 * EPS, ["const"])
        P.memset("dve", C.one1[:, :], 1.0, ["const"])
        P.dma("sp", C.ones[:, :], consts[:, 0:128], writes=["const"], semkey="c0")
        P.dma("sp", C.bd[:, :], consts[:, 128:256], writes=["const"], semkey="c1")

        def small(name, ap, shape, dt=F32, q="sp"):
            t = sb(name, shape, dt)
            P.dma(q, t[tuple(slice(None) for _ in shape)], ap, writes=["const"], semkey="c_" + name)
            return t

        if mode in (2, 3):
            for j in range(2):
                ffn[j]["g_sb"] = small("ffn_g_sb%d" % j, ffn[j]["g"], [128, 8])
                ffn[j]["cw_sb"] = small("ffn_cw_sb%d" % j, ffn[j]["cw"], [128, 44, 3])
            cv["g_sb"] = small("cv_g_sb", cv["g"], [128, 8])
            cv["ck_sb"] = small("cv_k_sb", cv["ck"], [128, 8, 3])
        if mode in (1, 2):
            at["g_sb"] = small("at_g_sb", at["g"], [128, 8])
            at["gqk_sb"] = small("at_gqk_sb", at["gqk"], [128, 16])
            at["fb_sb"] = small("at_fb_sb", at["fb"], [8, 1])
            at["wf_sb"] = small("at_wf_sb", at["w_in"].rearrange("(k p) c -> p k c", p=128)[:, :, 3072:3080],
                                [128, 8, 8], BF16, q="pool")
        for c in range(8):
            P.dma("sp", C.hT[:, c, :], hin[128 * c:128 * (c + 1), :], writes=[("h", c, ti) for ti in range(6)],
                  semkey=("hload", c))
        if mode in (2, 3):
            t_outproj(C, oT, w_o)
            if upto >= 2:
                t_ffn(C, ffn[0]["g_sb"], ffn[0]["up"], ffn[0]["dn"], ffn[0]["cw_sb"])
            if upto >= 3:
                t_convmix(C, cv["g_sb"], cv["w_in"], cv["ck_sb"], cv["w_out"])
            if upto >= 4:
                t_ffn(C, ffn[1]["g_sb"], ffn[1]["up"], ffn[1]["dn"], ffn[1]["cw_sb"])
            for c in range(8):
                P.dma("sp", hout[128 * c:128 * (c + 1), :], C.hT[:, c, PAD:NCOL], reads=[("h", c, ti) for ti in range(6)],
                      semkey=("hstore", c))
        if mode in (1, 2):
            t_qkv(C, at["g_sb"], at["w_in"], at["gqk_sb"], at["fb_sb"], at["wf_sb"], QT, KT, V, LF)
        P.emit()
    return nc


QT_W = 512
NQT = S // QT_W
NKB = S // 128
FCH = 512
NFC = S // FCH
ND_WARM = 1
D2, D3 = 2, 4


def build_A():
    nc = bass.Bass("TRN2", target_bir_lowering=False)
    din = lambda name, shape, dt=F32: nc.dram_tensor(name, shape, dt, kind="ExternalInput").ap()
    qf = din("qf", [128, S], BF16)
    kf = din("kf", [128, S], BF16)
    vf = din("vf", [S, 128], BF16)
    lf = din("lf", [2, S])
    qs = din("qs", [128, S], BF16)
    ks = din("ks", [128, S], BF16)
    vs = din("vs", [S, 128], BF16)
    cbf = din("acst_bf", [128, 256], BF16)
    cf32 = din("acst_f32", [128, 2 * 4 * 512])
    ctri = din("acst_tri", [32, 32])
    oT = nc.dram_tensor("oT", [256, S], BF16, kind="ExternalOutput").ap()

    with ExitStack() as stack:
        P = Prog(nc, stack)
        sb = lambda name, shape, dt: stack.enter_context(nc.sbuf_tensor(name, shape, dt))
        Qa = [sb("Qa%d" % i, [128, S], BF16) for i in range(2)]
        Ka = [sb("Ka%d" % i, [128, S], BF16) for i in range(2)]
        Va = [sb("Va%d" % i, [128, NKB, 65], BF16) for i in range(2)]
        Qs = sb("Qs", [128, S], BF16)
        Ks = sb("Ks", [128, S], BF16)
        Vs = sb("Vs", [128, NKB, 128], BF16)
        negtri = sb("negtri", [128, 128], BF16)
        negones = sb("negones", [128, 128], BF16)
        masks = sb("masks", [128, 8, 512], F32)
        onesf = sb("onesf", [128, 512], F32)
        tri32 = sb("tri32", [32, 32], F32)
        offs = sb("offs", [32, 2], F32)
        NPM, NTMF, NE1, NSP, NSS, NEG, NWT = 4, 2, 6, 5, 6, 6, 4
        pm = [sb("pm%d" % i, [128, 512], BF16) for i in range(NPM)]
        tmf = [sb("tmf%d" % i, [128, 512], F32) for i in range(NTMF)]
        ex = [sb("ex_%d" % i, [128, 2, 512], F32) for i in range(NE1)]
        e1 = [t[:, 0, :] for t in ex]
        eG = [t[:, 1, :] for t in ex]
        sp = [sb("sp_%d" % i, [128, 512], BF16) for i in range(NSP)]
        ssum = [sb("ssum_%d" % i, [128, 512], BF16) for i in range(NSS)]
        wt = [sb("wt_%d" % i, [128, 512], BF16) for i in range(NWT)]
        den = sb("den", [128, 512], F32)
        rden = sb("rden", [64, 512], F32)
        ost = [sb("ost%d" % i, [64, 512], BF16) for i in range(4)]
        psall = stack.enter_context(nc.psum_tensor("psall", [128, 8, 512], F32))
        ps = [psall[:, i, :] for i in range(8)]

        P.dma("sp", negtri[:, :], cbf[:, 0:128], writes=["c_tri"])
        P.dma("sp", negones[:, :], cbf[:, 128:256], writes=["c_ones"])
        P.dma("sp", tri32[:, :], ctri[:, :], writes=["c_tri32"])
        P.memset("dve", onesf[:, :], 1.0, ["c_onesf"])
        dummy = sb("warm_rhs", [128, 512], BF16)
        P.memset("dve", dummy[:, :], 0.0, ["c_dummy"])

        vvf = vf.rearrange("(b p) d -> p b d", p=128)
        for hh in range(2):
            P.dma("sp", Qa[hh][0:64, :], qf[64 * hh:64 * hh + 64, :], writes=[("Qf", hh)])
            P.dma("sp", Ka[hh][0:64, :], kf[64 * hh:64 * hh + 64, :], writes=[("Kf", hh)])
            for half in range(2):
                P.dma("sp", Va[hh][:, 32 * half:32 * half + 32, 0:64], vvf[:, 32 * half:32 * half + 32, 64 * hh:64 * hh + 64],
                      writes=[("Vf", hh)], semkey=("Vf", hh))
            P.memset("dve", Va[hh][:, :, 64:65], 1.0, [("V1", hh)])
            P.memset("dve", Qa[hh][64:70, :], 1.0, [("Qx", hh)])
            P.memset("dve", Ka[hh][64:70, :], 1.0, [("Kx", hh)])
            if hh == 0:
                P.dma("sp", Qs[:, :], qs[:, :], writes=["Qs"])
                P.dma("sp", Ks[:, :], ks[:, :], writes=["Ks"])
                vvs = vs.rearrange("(b p) d -> p b d", p=128)
                for half in range(2):
                    P.dma("sp", Vs[:, 32 * half:32 * half + 32, :], vvs[:, 32 * half:32 * half + 32, :], writes=["Vs"],
                          semkey="Vs")
        P.dma("sp", masks[:, :, :], cf32.rearrange("p (a b) -> p a b", a=8), writes=["c_masks"])

        R = 2 * NFC
        lf32, lfk = e1[0][0:R, :], ("e1", 0)
        loc, lock = eG[0][0:R, :], ("eG", 0)
        Ft, Fk = tmf[0][0:R, :], ("tmf", 0)
        r1, r1k = tmf[1][0:R, :], ("tmf", 1)
        r2, r2k = e1[1][0:R, :], ("e1", 1)
        P.dma("sp", lf32, lf.rearrange("h (c n) -> (h c) n", n=FCH), writes=[lfk])
        P.op("dve", lambda e: e.tensor_tensor_scan(out=loc, data0=onesf[0:R, :], data1=lf32, initial=0.0,
                                                  op0=ALU.mult, op1=ALU.add), reads=[lfk, "c_onesf"], writes=[lock])
        P.mm(ps[4][0:R, 0:1], tri32[:, :], loc[:, FCH - 1:FCH], True, True, reads=[lock, "c_tri32"], writes=[("ps", 4)])
        P.act(offs[:, 0:1], ps[4][0:R, 0:1], AF.Copy, reads=[("ps", 4)], writes=["offs"])
        P.ts(Ft, loc, offs[:, 0:1], None, ALU.add, None, reads=[lock, "offs"], writes=[Fk])
        hi, hik = sp[0][0:R, :], ("sp", 0)
        mid, midk = sp[1][0:R, :], ("sp", 1)
        lo, lok = sp[2][0:R, :], ("sp", 2)
        P.copy("dve", hi, Ft, reads=[Fk], writes=[hik])
        P.tt(r1, Ft, hi, ALU.subtract, reads=[Fk, hik], writes=[r1k])
        P.copy("dve", mid, r1, reads=[r1k], writes=[midk])
        P.tt(r2, r1, mid, ALU.subtract, reads=[r1k, midk], writes=[r2k])
        P.copy("dve", lo, r2, reads=[r2k], writes=[lok])
        negs = []
        for j, (b, bk) in enumerate(((hi, hik), (mid, midk), (lo, lok))):
            nb_, nbk = wt[j][0:R, :], ("wt", j)
            P.ts(nb_, b, -1.0, None, ALU.mult, None, reads=[bk], writes=[nbk], eng="pool")
            negs.append((nb_, nbk))
        for hh in range(2):
            for j, ((b, bk), (nb_, nbk)) in enumerate(zip(((hi, hik), (mid, midk), (lo, lok)), negs)):
                for ch in range(NFC):
                    r = NFC * hh + ch
                    P.dma("sp", Qa[hh][64 + j:65 + j, ch * FCH:(ch + 1) * FCH], b[r:r + 1, :], reads=[bk, ("Qx", hh)],
                          writes=[("Qg", hh)], semkey=("Qg", hh))
                    P.dma("sp", Ka[hh][67 + j:68 + j, ch * FCH:(ch + 1) * FCH], nb_[r:r + 1, :], reads=[nbk, ("Kx", hh)],
                          writes=[("Kg", hh)], semkey=("Kg", hh))

        cnt = {"pm": 0, "tmf": 0, "e1": 0, "sp": 0, "ss": 0, "eG": 0, "wt": 0, "Ss": 0, "Sf": 0, "G": 0, "ost": 0}

        def nxt(name, n):
            i = cnt[name]
            cnt[name] = i + 1
            return i % n

        for pair in range(2):
            fq = [("Qf", pair), ("Qg", pair), ("Qx", pair)]
            fk = [("Kf", pair), ("Kg", pair), ("Kx", pair)]
            fv = [("Vf", pair), ("V1", pair)]
            r0 = 64 * pair
            sbq = Qs[r0:r0 + 64, :]
            sbk = Ks[r0:r0 + 64, :]
            fblocks = [(qt, kb) for qt in range(NQT) for kb in range(4 * (qt + 1))]
            sblocks = [(qt, kb) for qt in range(NQT) for kb in range(4 * (qt + 1) - 1, -1, -1)]
            NBL = len(fblocks)
            fst = {}
            sst = {}
            pending = []
            lastG = [1]

            def fox_qk(g):
                qt, kb = fblocks[g]
                si = 4 + nxt("Sf", 2)
                fst[g] = dict(S=si)
                P.mm(ps[si][:, :], Ka[pair][0:70, kb * 128:(kb + 1) * 128], Qa[pair][0:70, qt * QT_W:(qt + 1) * QT_W],
                     True, True, reads=fq + fk, writes=[("ps", si)])

            def fox_exp(g):
                qt, kb = fblocks[g]
                si = fst[g]["S"]
                j = kb - 4 * qt
                pi = nxt("pm", NPM)
                fst[g]["pm"] = pi
                if j >= 0:
                    ti = nxt("tmf", NTMF)
                    P.tt(tmf[ti][:, :], ps[si][:, :], masks[:, j, :], ALU.add, reads=[("ps", si), "c_masks"],
                         writes=[("tmf", ti)])
                    P.act(pm[pi][:, :], tmf[ti][:, :], AF.Exp, reads=[("tmf", ti)], writes=[("pm", pi)])
                else:
                    P.act(pm[pi][:, :], ps[si][:, :], AF.Exp, reads=[("ps", si)], writes=[("pm", pi)])

            def fox_pv(g):
                qt, kb = fblocks[g]
                nkb = 4 * (qt + 1)
                pi = fst.pop(g)["pm"]
                P.mm(ps[6][0:65, :], Va[pair][:, kb, 0:65], pm[pi][:, :], kb == 0, kb == nkb - 1,
                     reads=fv + [("pm", pi)], writes=[("ps", 6)])
                if kb == nkb - 1:
                    pending.append(qt)

            def fox_epilogue(qt):
                q0 = qt * QT_W
                gi = lastG[0]
                P.act(den[64:65, :], ps[6][64:65, :], AF.Copy, reads=[("ps", 6)], writes=["den"])
                P.mm(ps[gi][0:64, :], onesf[64:65, 0:64], den[64:65, :], True, True, reads=["den", "c_onesf"],
                     writes=[("ps", gi)])
                P.recip(rden[:, :], ps[gi][0:64, :], reads=[("ps", gi)], writes=["rden"])
                oi = nxt("ost", 4)
                P.tt(ost[oi][:, :], ps[6][0:64, :], rden[:, :], ALU.mult, reads=[("ps", 6), "rden"],
                     writes=[("ost", oi)])
                P.dma("sp", oT[r0:r0 + 64, q0:q0 + QT_W], ost[oi][:, :], reads=[("ost", oi)])

            def sb_qk(g):
                qt, kb = sblocks[g]
                si = 2 * (g % 2)
                sst.setdefault(g, {})["S"] = si
                P.mm(ps[si][:, :], sbk[:, kb * 128:(kb + 1) * 128], sbq[:, qt * QT_W:(qt + 1) * QT_W], True, True,
                     reads=["Qs", "Ks"], writes=[("ps", si)])

            def sb_exp(g, merged):
                qt, kb = sblocks[g]
                d = sst[g]
                si = d["S"]
                ei = g % NE1
                d["e1"] = ei
                if merged:
                    P.act(ex[ei][:, :, :], psall[:, si:si + 2, :], AF.Exp, reads=[("ps", si), ("ps", si + 1)],
                          writes=[("e1", ei), ("eG", ei)])
                else:
                    P.act(e1[ei][:, :], ps[si][:, :], AF.Exp, reads=[("ps", si)], writes=[("e1", ei)])
                j = kb - 4 * qt
                if j >= 0:
                    P.tt(e1[ei][:, :], e1[ei][:, :], masks[:, 4 + j, :], ALU.mult, reads=[("e1", ei), "c_masks"],
                         writes=[("e1", ei)], eng="pool")

            def sb_ln(g):
                qt, kb = sblocks[g]
                nkb = 4 * (qt + 1)
                k = nkb - 1 - kb
                d = sst[g]
                ei = d["e1"]
                spi = nxt("sp", NSP)
                d["sp"] = spi
                P.act(sp[spi][:, :], e1[ei][:, :], AF.Ln, reads=[("e1", ei), "c_onesf"], writes=[("sp", spi)],
                      bias=onesf[:, 0:1], scale=1.0)
                if k + 1 < nkb:
                    nsi = nxt("ss", NSS)
                    sst.setdefault(g + 1, {})["ss"] = nsi
                    if k == 0:
                        P.copy("pool", ssum[nsi][:, :], sp[spi][:, :], reads=[("sp", spi)], writes=[("ss", nsi)])
                    else:
                        ci = d["ss"]
                        P.tt(ssum[nsi][:, :], ssum[ci][:, :], sp[spi][:, :], ALU.add,
                             reads=[("ss", ci), ("sp", spi)], writes=[("ss", nsi)], eng="pool")

            def sb_s2a(g):
                qt, kb = sblocks[g]
                nkb = 4 * (qt + 1)
                k = nkb - 1 - kb
                d = sst[g]
                gi = 2 * ((g + 1) % 2) + 1
                d["G"] = gi
                spi = d["sp"]
                P.mm(ps[gi][:, :], negtri[:, :], sp[spi][:, :], True, k == 0, reads=[("sp", spi), "c_tri"],
                     writes=[("ps", gi)])
                if k > 0:
                    ci = d["ss"]
                    P.mm(ps[gi][:, :], negones[:, :], ssum[ci][:, :], False, True, reads=[("ss", ci), "c_ones"],
                         writes=[("ps", gi)])

            def sb_s2b(g, merged):
                d = sst[g]
                gi = d["G"]
                lastG[0] = gi
                gi2 = (g + 3) % NE1
                if not merged:
                    P.act(eG[gi2][:, :], ps[gi][:, :], AF.Exp, reads=[("ps", gi)], writes=[("eG", gi2)])
                wi = nxt("wt", NWT)
                d["wt"] = wi
                P.tt(wt[wi][:, :], e1[d["e1"]][:, :], eG[gi2][:, :], ALU.mult, reads=[("e1", d["e1"]), ("eG", gi2)],
                     writes=[("wt", wi)])

            def sb_s3(g):
                qt, kb = sblocks[g]
                nkb = 4 * (qt + 1)
                k = nkb - 1 - kb
                d = sst.pop(g)
                wi = d["wt"]
                P.mm(ps[7][0:64, :], Vs[:, kb, r0:r0 + 64], wt[wi][:, :], k == 0, k == nkb - 1,
                     reads=["Vs", ("wt", wi)], writes=[("ps", 7)])
                if k == nkb - 1:
                    q0 = qt * QT_W
                    oi = nxt("ost", 4)
                    P.act(ost[oi][:, :], ps[7][0:64, :], AF.Copy, reads=[("ps", 7)], writes=[("ost", oi)])
                    P.dma("sp", oT[128 + r0:128 + r0 + 64, q0:q0 + QT_W], ost[oi][:, :], reads=[("ost", oi)])

            ok = lambda x: 0 <= x < NBL
            for g in range(-1, NBL + 5):
                if ok(g + 1):
                    sb_qk(g + 1)
                if ok(g):
                    fox_qk(g)
                if ok(g - 2):
                    fox_pv(g - 2)
                    sb_s2a(g - 2)
                if ok(g - 4):
                    sb_s3(g - 4)
                for _ in range(ND_WARM):
                    P.mm(ps[7][64:128, :], negones[:, 0:64], dummy[:, :], True, True, reads=["c_ones", "c_dummy"], writes=())
                merged = ok(g) and ok(g - 3)
                if ok(g):
                    sb_exp(g, merged)
                if ok(g - 1):
                    fox_exp(g - 1)
                if ok(g):
                    sb_ln(g)
                if ok(g - 3):
                    sb_s2b(g - 3, merged)
                while pending:
                    fox_epilogue(pending.pop(0))
        P.emit()
    return nc


def _bf(a):
    return np.ascontiguousarray(a).astype(ml_dtypes.bfloat16)


def _cols(vec, nchunk):
    return np.ascontiguousarray(np.asarray(vec, np.float32).reshape(nchunk, 128).T)


def _taps(w, nchunk):
    return np.ascontiguousarray(np.asarray(w, np.float32).T.reshape(nchunk, 128, 3).transpose(1, 0, 2))


def _t_consts():
    ones = np.ones((128, 128), np.float32)
    bd = np.zeros((128, 128), np.float32)
    bd[:64, :64] = 1.0
    bd[64:, 64:] = 1.0
    return _bf(np.concatenate([ones, bd], axis=1))


def _a_consts():
    j = np.arange(128)[:, None]
    s = np.arange(128)[None, :]
    negtri = np.where(j >= s, -1.0, 0.0).astype(np.float32)
    negones = -np.ones((128, 128), np.float32)
    cbf = _bf(np.concatenate([negtri, negones], axis=1))
    sl = np.arange(128)[:, None]
    tl = np.arange(512)[None, :]
    madd = np.stack([np.where(128 * jj + sl <= tl, 0.0, -30000.0) for jj in range(4)], axis=1)
    m01 = np.stack([np.where(128 * jj + sl < tl, 1.0, 0.0) for jj in range(4)], axis=1)
    cf = np.concatenate([madd, m01], axis=1).astype(np.float32).reshape(128, 8 * 512)
    pp = np.arange(32)
    tri = ((pp[:, None] < pp[None, :]) & (pp[:, None] // 16 == pp[None, :] // 16)).astype(np.float32)
    return cbf, (np.ascontiguousarray(cf), np.ascontiguousarray(tri))


def _pad_cols(full, j, dtype):
    out = np.zeros((full.shape[0], NCOL), dtype)
    t0 = NTOK * j
    out[:, PAD:] = full[:, t0:t0 + NTOK]
    if j > 0:
        out[:, 2:PAD] = full[:, t0 - 6:t0]
    return out


def _attn_inputs(inp, i):
    gq = np.concatenate([np.tile(inp["fox_q_gain"][i], 8), np.tile(inp["sb_q_gain"][i], 8)])
    gk = np.concatenate([np.tile(inp["fox_k_gain"][i], 8), np.tile(inp["sb_k_gain"][i], 8)])
    return {
        "at_g": _cols(inp["attn_norm"][i], 8),
        "at_in": np.ascontiguousarray(inp["attn_w_in"][i], dtype=np.float32),
        "at_gqk": np.ascontiguousarray(np.concatenate([_cols(gq, 8), _cols(gk, 8)], axis=1)),
        "at_fb": np.ascontiguousarray(np.asarray(inp["attn_f_bias"][i], np.float32).reshape(8, 1)),
    }


def _block_inputs(inp, i):
    d = {"w_o": np.ascontiguousarray(inp["attn_w_out"][i], dtype=np.float32)}
    for jj, l in enumerate((2 * i, 2 * i + 1)):
        d["ffn_g%d" % jj] = _cols(inp["ffn_norm"][l], 8)
        d["ffn_up%d" % jj] = np.ascontiguousarray(inp["ffn_w_up"][l], dtype=np.float32)
        d["ffn_cw%d" % jj] = _taps(inp["ffn_conv"][l], 44)
        d["ffn_dn%d" % jj] = np.ascontiguousarray(inp["ffn_w_down"][l], dtype=np.float32)
    d["cv_g"] = _cols(inp["conv_norm"][i], 8)
    d["cv_in"] = np.ascontiguousarray(inp["conv_w_in"][i], dtype=np.float32)
    d["cv_k"] = _taps(inp["conv_kernel"][i], 8)
    d["cv_out"] = np.ascontiguousarray(inp["conv_w_out"][i], dtype=np.float32)
    return d


def _run(nc, in_maps):
    res = run_bass_kernel_spmd(nc, in_maps, core_ids=list(range(8)))
    return res.results


def _gather_fm(results, name):
    return [np.concatenate([np.asarray(results[4 * b + j][name]) for j in range(4)], axis=1) for b in range(NB)]


def _attention(results, cbf, cf):
    QT = _gather_fm(results, "QT")
    KT = _gather_fm(results, "KT")
    LF = _gather_fm(results, "LF")
    V = [np.concatenate([np.asarray(results[4 * b + j]["V"]) for j in range(4)], axis=0) for b in range(NB)]
    maps = []
    for c in range(8):
        b, g = divmod(c, 4)
        f0, s0 = 128 * g, 512 + 128 * g
        maps.append({
            "qf": np.ascontiguousarray(QT[b][f0:f0 + 128]), "kf": np.ascontiguousarray(KT[b][f0:f0 + 128]),
            "vf": np.ascontiguousarray(V[b][:, f0:f0 + 128]), "lf": np.ascontiguousarray(LF[b][2 * g:2 * g + 2]),
            "qs": np.ascontiguousarray(QT[b][s0:s0 + 128]), "ks": np.ascontiguousarray(KT[b][s0:s0 + 128]),
            "vs": np.ascontiguousarray(V[b][:, s0:s0 + 128]), "acst_bf": cbf, "acst_f32": cf[0], "acst_tri": cf[1],
        })
    ares = _run(build_A(), maps)
    oT = []
    for b in range(NB):
        o = np.zeros((D, S), ml_dtypes.bfloat16)
        for g in range(4):
            r = np.asarray(ares[4 * b + g]["oT"])
            o[128 * g:128 * g + 128] = r[0:128]
            o[512 + 128 * g:512 + 128 * g + 128] = r[128:256]
        oT.append(o)
    return oT


def kernel(**inp):
    inp = {k: np.asarray(v) for k, v in inp.items()}
    x = inp["x"].astype(np.float32, copy=False)
    tc = _t_consts()
    cbf, cf = _a_consts()
    hfull = [np.ascontiguousarray(x[b].T) for b in range(NB)]
    maps = []
    for c in range(8):
        b, j = divmod(c, 4)
        m = {"hin": _pad_cols(hfull[b], j, np.float32), "cst_bf": tc}
        m.update(_attn_inputs(inp, 0))
        maps.append(m)
    r1 = _run(build_T(1), maps)
    oT = _attention(r1, cbf, cf)
    maps = []
    for c in range(8):
        b, j = divmod(c, 4)
        m = {"hin": _pad_cols(hfull[b], j, np.float32), "oT": _pad_cols(oT[b], j, ml_dtypes.bfloat16), "cst_bf": tc}
        m.update(_block_inputs(inp, 0))
        m.update(_attn_inputs(inp, 1))
        maps.append(m)
    r2 = _run(build_T(2), maps)
    hfull = _gather_fm(r2, "hout")
    oT = _attention(r2, cbf, cf)
    maps = []
    for c in range(8):
        b, j = divmod(c, 4)
        m = {"hin": _pad_cols(hfull[b], j, np.float32), "oT": _pad_cols(oT[b], j, ml_dtypes.bfloat16), "cst_bf": tc}
        m.update(_block_inputs(inp, 1))
        maps.append(m)
    r3 = _run(build_T(3), maps)
    hfull = _gather_fm(r3, "hout")
    out = np.stack([np.ascontiguousarray(hfull[b].T) for b in range(NB)], axis=0)
    return out.astype(np.float32)
```

```python
import numpy as np
import ml_dtypes
from contextlib import ExitStack
import concourse.bass as bass
import concourse.mybir as mybir
from concourse.bass_utils import run_bass_kernel_spmd

F32 = mybir.dt.float32
BF16 = mybir.dt.bfloat16
AF = mybir.ActivationFunctionType
ALU = mybir.AluOpType

D = 1024
S = 8192
NB = 2
DFF = 2816
EPS = 1e-6
NTOK = 2048
PAD = 8
NCOL = NTOK + PAD
TILES = [(2, 346), (346, 690), (690, 1032), (1032, 1374), (1374, 1716), (1716, 2056)]
GROUPS = [(0, 1, 2), (3, 4, 5)]
GBASE = [2, 1032]
GW = 1032
A_LOFF = [0, 344, 688]
TW = 346
NPS = 8


class Prog:
    ENGS = ("sp", "pe", "act", "dve", "pool")

    def __init__(self, nc, stack):
        self.nc = nc
        self.stack = stack
        self.streams = {e: [] for e in self.ENGS}
        self.esem = {}
        self.ecnt = {}
        self.sid = 0
        for e in ("pe", "act", "dve", "pool"):
            self.esem[e] = (self._sid(), stack.enter_context(nc.semaphore("s_" + e)))
            self.ecnt[e] = 0
        self.waited = {e: {} for e in self.ENGS}
        self.res = {}
        self.dsem = {}
        self.ps_i = 0
        self.tmp_i = 0

    def _sid(self):
        self.sid += 1
        return self.sid

    def _deps(self, eng, reads, writes):
        need = {}
        wd = self.waited[eng]

        def add(ev):
            if ev is None:
                return
            sid, sem, val, src = ev
            if eng == "pe" and src == "pe":
                return
            if wd.get(sid, 0) >= val:
                return
            if sid not in need or need[sid][1] < val:
                need[sid] = (sem, val)

        for r in reads:
            st = self.res.get(r)
            if st is not None:
                add(st[0])
        for w in writes:
            st = self.res.get(w)
            if st is not None:
                add(st[0])
                for ev in st[1].values():
                    add(ev)
        for sid, (sem, val) in need.items():
            wd[sid] = val
        return list(need.values())

    def _record(self, ev, reads, writes):
        for w in writes:
            self.res[w] = [ev, {}]
        ws = set(writes)
        for r in reads:
            if r in ws:
                continue
            st = self.res.get(r)
            if st is None:
                st = [None, {}]
                self.res[r] = st
            st[1][ev[0]] = ev

    def op(self, eng, fn, reads=(), writes=()):
        waits = self._deps(eng, reads, writes)
        self.ecnt[eng] += 1
        sid, sem = self.esem[eng]
        ev = (sid, sem, self.ecnt[eng], eng)
        self.streams[eng].append((waits, fn, (sem, 1)))
        self._record(ev, reads, writes)

    def dma(self, q, out, in_, reads=(), writes=(), semkey=None):
        waits = self._deps(q, reads, writes)
        key = semkey if semkey is not None else (writes[0] if writes else ("st", reads[0]))
        ent = self.dsem.get(key)
        if ent is None:
            ent = [self._sid(), self.stack.enter_context(self.nc.semaphore("d%d" % len(self.dsem))), 0]
            self.dsem[key] = ent
        ent[2] += 16
        ev = (ent[0], ent[1], ent[2], "dma")
        self.streams[q].append((waits, lambda e: e.dma_start(out=out, in_=in_), (ent[1], 16)))
        self._record(ev, reads, writes)

    def mm(self, out, lhsT, rhs, start, stop, reads, writes):
        self.op("pe", lambda e: e.matmul(out, lhsT=lhsT, rhs=rhs, start=start, stop=stop), reads, writes)

    def act(self, out, in_, func, reads, writes, **kw):
        self.op("act", lambda e: e.activation(out=out, in_=in_, func=func, **kw), reads, writes)

    def stt(self, out, in0, scalar, in1, op0, op1, reads, writes, eng="dve"):
        self.op(eng, lambda e: e.scalar_tensor_tensor(out=out, in0=in0, scalar=scalar, in1=in1, op0=op0, op1=op1),
                reads, writes)

    def tt(self, out, in0, in1, op, reads, writes, eng="dve"):
        self.op(eng, lambda e: e.tensor_tensor(out=out, in0=in0, in1=in1, op=op), reads, writes)

    def ts(self, out, in0, s1, s2, op0, op1, reads, writes, eng="dve"):
        if op1 is None:
            self.op(eng, lambda e: e.tensor_scalar(out=out, in0=in0, scalar1=s1, scalar2=None, op0=op0), reads, writes)
        else:
            self.op(eng, lambda e: e.tensor_scalar(out=out, in0=in0, scalar1=s1, scalar2=s2, op0=op0, op1=op1),
                    reads, writes)

    def recip(self, out, in_, reads, writes):
        self.op("dve", lambda e: e.reciprocal(out=out, in_=in_), reads, writes)

    def copy(self, eng, out, in_, reads, writes):
        self.op(eng, lambda e: e.tensor_copy(out=out, in_=in_), reads, writes)

    def memset(self, eng, ap, val, writes):
        self.op(eng, lambda e: e.memset(ap, val), (), writes)

    def emit(self):
        nc = self.nc
        fin = [(ent[1], ent[2]) for ent in self.dsem.values()]
        with nc.Block() as block:
            for eng, reg in (("sp", block.sync), ("pe", block.tensor), ("act", block.scalar),
                             ("dve", block.vector), ("pool", block.gpsimd)):
                items = self.streams[eng]

                def body(e, items=items, eng=eng):
                    for waits, fn, inc in items:
                        for sem, val in waits:
                            e.wait_ge(sem, val)
                        fn(e).then_inc(inc[0], inc[1])
                    if eng == "sp":
                        for sem, val in fin:
                            e.wait_ge(sem, val)

                reg(body)


class TCtx:
    pass


def _wview(W, c0, w):
    return W.rearrange("(k p) c -> p k c", p=128)[:, :, c0:c0 + w]


def t_alloc(nc, stack, P):
    C = TCtx()
    C.P = P
    sb = lambda name, shape, dt: stack.enter_context(nc.sbuf_tensor(name, shape, dt))
    C.hT = sb("hT", [128, 8, NCOL], F32)
    C.xn = sb("xn", [128, 8, NCOL], BF16)
    C.a = sb("abuf", [128, 22, GW], BF16)
    C.wbuf = [sb("wbuf%d" % i, [128, 6144], BF16) for i in range(2)]
    C.sq = [sb("sq%d" % i, [128, 8, TW], BF16) for i in range(2)]
    C.tmp = [sb("tmp%d" % i, [128, TW], F32) for i in range(8)]
    C.stg = [sb("stg%d" % i, [128, 512], BF16) for i in range(6)]
    C.lf = [sb("lf%d" % i, [8, TW], F32) for i in range(6)]
    C.ones = sb("ones_bf", [128, 128], BF16)
    C.bd = sb("bd_bf", [128, 128], BF16)
    C.ps = [stack.enter_context(nc.psum_tensor("ps%d" % i, [128, 512], F32)) for i in range(NPS)]
    C.wi = 0
    C.sqi = 0
    C.tmpi = 0
    C.psi = 0
    C.stgi = 0
    C.lfi = 0
    return C


def _ps(C):
    i = C.psi
    C.psi = (i + 1) % NPS
    return C.ps[i], ("ps", i)


def _tmp(C):
    i = C.tmpi
    C.tmpi = (i + 1) % len(C.tmp)
    return C.tmp[i], ("tmp", i)


def _stg(C):
    i = C.stgi
    C.stgi = (i + 1) % len(C.stg)
    return C.stg[i], ("stg", i)


def _lf(C):
    i = C.lfi
    C.lfi = (i + 1) % len(C.lf)
    return C.lf[i], ("lf", i)


def _loadw(C, parts, kdim):
    P = C.P
    i = C.wi
    C.wi = (i + 1) % 2
    wt = C.wbuf[i]
    tot = sum(w for _, _, w in parts)
    view = wt[:, 0:kdim * tot].rearrange("p (k c) -> p k c", k=kdim)
    off = 0
    for W, c0, w in parts:
        P.dma("pool", view[:, :, off:off + w], _wview(W, c0, w), reads=(), writes=[("w", i)], semkey=("w", i))
        off += w
    return view, ("w", i)


def t_norm(C, gcol, tiles=range(6)):
    P = C.P
    for ti in tiles:
        s, e = TILES[ti]
        s0 = 0 if ti == 0 else s
        n = e - s0
        sq = C.sq[C.sqi]
        sqk = ("sq", C.sqi)
        C.sqi = (C.sqi + 1) % 2
        hk = [("h", c, ti) for c in range(8)]
        P.act(sq[:, :, 0:n], C.hT[:, :, s0:e], AF.Square, reads=hk, writes=[sqk])
        ps, psk = _ps(C)
        for c in range(8):
            P.mm(ps[:, 0:n], C.ones[:, :], sq[:, c, 0:n], c == 0, c == 7, reads=[sqk, "const"], writes=[psk])
        rt, rtk = _tmp(C)
        P.act(rt[:, 0:n], ps[:, 0:n], AF.Sqrt, reads=[psk], writes=[rtk], scale=1.0 / D, bias=C.eps1[:, 0:1])
        P.recip(rt[:, 0:n], rt[:, 0:n], reads=[rtk], writes=[rtk])
        for c in range(8):
            P.stt(C.xn[:, c, s0:e], C.hT[:, c, s0:e], gcol[:, c:c + 1], rt[:, 0:n], ALU.mult, ALU.mult,
                  reads=[("h", c, ti), rtk, "const"], writes=[("xn", c, ti)])


def _xnk(ti, halo=True):
    ks = [("xn", c, ti) for c in range(8)]
    if halo and ti > 0:
        ks += [("xn", c, ti - 1) for c in range(8)]
    return ks


def _conv3(C, ps, psk, n, wcol, j):
    P = C.P
    t, tk = _tmp(C)
    P.act(t[:, 0:n], ps[:, 2:n + 2], AF.Identity, reads=[psk, "const"], writes=[tk], scale=wcol[:, j, 2:3])
    P.stt(t[:, 0:n], ps[:, 1:n + 1], wcol[:, j, 1:2], t[:, 0:n], ALU.mult, ALU.add, reads=[psk, tk, "const"], writes=[tk])
    P.stt(t[:, 0:n], ps[:, 0:n], wcol[:, j, 0:1], t[:, 0:n], ALU.mult, ALU.add, reads=[psk, tk, "const"], writes=[tk])
    return t, tk


def t_proj_down(C, W, kdim, gi):
    P = C.P
    for ob in range(4):
        wt, wk = _loadw(C, [(W, 256 * ob, 256)], kdim)
        for ti in GROUPS[gi]:
            s, e = TILES[ti]
            n = e - s
            ls = A_LOFF[ti % 3]
            for jj in range(2):
                jc = 2 * ob + jj
                ps, psk = _ps(C)
                for k in range(kdim):
                    P.mm(ps[:, 0:n], wt[:, k, jj * 128:(jj + 1) * 128], C.a[:, k, ls:ls + n], k == 0, k == kdim - 1,
                         reads=[wk, ("a", k, ti % 3)], writes=[psk])
                P.tt(C.hT[:, jc, s:e], ps[:, 0:n], C.hT[:, jc, s:e], ALU.add, reads=[psk, ("h", jc, ti)],
                     writes=[("h", jc, ti)])


def t_outproj(C, oT, W):
    P = C.P
    for gi in range(2):
        for ti in GROUPS[gi]:
            s, e = TILES[ti]
            ls = A_LOFF[ti % 3]
            P.dma("sp", C.a[:, 0:8, ls:ls + (e - s)], oT.rearrange("(k p) c -> p k c", p=128)[:, :, s:e],
                  reads=(), writes=[("a", k, ti % 3) for k in range(8)], semkey=("aload", ti))
        t_proj_down(C, W, 8, gi)


def t_ffn(C, gcol, W_up, W_dn, cw):
    P = C.P
    t_norm(C, gcol)
    for gi in range(2):
        for ub in range(11):
            wt, wk = _loadw(C, [(W_up, 256 * ub, 256), (W_up, DFF + 256 * ub, 256)], 8)
            for ti in GROUPS[gi]:
                s, e = TILES[ti]
                n = e - s
                ls = A_LOFF[ti % 3]
                xk = _xnk(ti)
                for pj in range(2):
                    i = 2 * ub + pj
                    gp, gk = _ps(C)
                    vp, vk = _ps(C)
                    for kc in range(8):
                        P.mm(gp[:, 0:n + 2], wt[:, kc, pj * 128:(pj + 1) * 128], C.xn[:, kc, s - 2:e], kc == 0, kc == 7,
                             reads=[wk] + xk, writes=[gk])
                    for kc in range(8):
                        P.mm(vp[:, 0:n + 2], wt[:, kc, 256 + pj * 128:256 + (pj + 1) * 128], C.xn[:, kc, s - 2:e],
                             kc == 0, kc == 7, reads=[wk] + xk, writes=[vk])
                    t1, t1k = _conv3(C, gp, gk, n, cw, i)
                    t2, t2k = _conv3(C, vp, vk, n, cw, 22 + i)
                    P.act(t1[:, 0:n], t1[:, 0:n], AF.Silu, reads=[t1k], writes=[t1k])
                    P.tt(C.a[:, i, ls:ls + n], t1[:, 0:n], t2[:, 0:n], ALU.mult, reads=[t1k, t2k],
                         writes=[("a", i, ti % 3)])
        t_proj_down(C, W_dn, 22, gi)


def t_convmix(C, gcol, W_in, ck, W_out):
    P = C.P
    t_norm(C, gcol)
    for gi in range(2):
        for cb in range(4):
            wt, wk = _loadw(C, [(W_in, 256 * cb, 256), (W_in, 1024 + 256 * cb, 256), (W_in, 2048 + 256 * cb, 256)], 8)
            for ti in GROUPS[gi]:
                s, e = TILES[ti]
                n = e - s
                ls = A_LOFF[ti % 3]
                xk = _xnk(ti)
                for cj in range(2):
                    i = 2 * cb + cj
                    pss = []
                    for part in range(3):
                        pp, pk = _ps(C)
                        for kc in range(8):
                            P.mm(pp[:, 0:n + 2], wt[:, kc, 256 * part + cj * 128:256 * part + (cj + 1) * 128],
                                 C.xn[:, kc, s - 2:e], kc == 0, kc == 7, reads=[wk] + xk, writes=[pk])
                        pss.append((pp, pk))
                    (bp, bk), (cp, ck_), (up, uk) = pss
                    tc_, tck = _tmp(C)
                    P.act(tc_[:, 0:n + 2], cp[:, 0:n + 2], AF.Copy, reads=[ck_], writes=[tck])
                    P.tt(tc_[:, 0:n + 2], up[:, 0:n + 2], tc_[:, 0:n + 2], ALU.mult, reads=[uk, tck], writes=[tck])
                    t3, t3k = _conv3(C, tc_, tck, n, ck, i)
                    P.tt(C.a[:, i, ls:ls + n], bp[:, 2:n + 2], t3[:, 0:n], ALU.mult, reads=[bk, t3k],
                         writes=[("a", i, ti % 3)])
        t_proj_down(C, W_out, 8, gi)


def t_qkv(C, gcol, W_in, gqk, fb, wf, QT, KT, V, LF):
    P = C.P
    t_norm(C, gcol)
    for gi in range(2):
        for qb in range(8):
            wt, wk = _loadw(C, [(W_in, 256 * qb, 256)], 8)
            for ti in GROUPS[gi]:
                s, e = TILES[ti]
                n = e - s
                xk = _xnk(ti, halo=False)
                so = max(s, PAD)
                for cj in range(2):
                    c = 2 * qb + cj
                    isq = c < 8
                    pp, pk = _ps(C)
                    for kc in range(8):
                        P.mm(pp[:, 0:n], wt[:, kc, cj * 128:(cj + 1) * 128], C.xn[:, kc, s:e], kc == 0, kc == 7,
                             reads=[wk] + xk, writes=[pk])
                    sq, sqk = _stg(C)
                    P.act(sq[:, 0:n], pp[:, 0:n], AF.Square, reads=[pk], writes=[sqk])
                    p2, p2k = _ps(C)
                    P.mm(p2[:, 0:n], C.bd[:, :], sq[:, 0:n], True, True, reads=[sqk, "const"], writes=[p2k])
                    rt, rtk = _tmp(C)
                    if isq:
                        P.act(rt[:, 0:n], p2[:, 0:n], AF.Sqrt, reads=[p2k], writes=[rtk], scale=1.0, bias=C.eps64[:, 0:1])
                    else:
                        P.act(rt[:, 0:n], p2[:, 0:n], AF.Sqrt, reads=[p2k], writes=[rtk], scale=1.0 / 64, bias=C.eps1[:, 0:1])
                    P.recip(rt[:, 0:n], rt[:, 0:n], reads=[rtk], writes=[rtk])
                    st, stk = _stg(C)
                    P.stt(st[:, 0:n], pp[:, 0:n], gqk[:, c:c + 1], rt[:, 0:n], ALU.mult, ALU.mult,
                          reads=[pk, rtk, "const"], writes=[stk])
                    dst = QT if isq else KT
                    cc = c % 8
                    P.dma("sp", dst[128 * cc:128 * (cc + 1), so - PAD:e - PAD], st[:, so - s:n], reads=[stk], writes=())
        for vb in range(2):
            wt, wk = _loadw(C, [(W_in, 2048 + 512 * vb, 512)], 8)
            for m in range(8):
                c0 = PAD + 1024 * gi + 128 * m
                tis = sorted({ti for ti in range(6) if TILES[ti][0] < c0 + 128 and TILES[ti][1] > c0})
                xk = [("xn", c, ti) for c in range(8) for ti in tis]
                pp, pk = _ps(C)
                for kc in range(8):
                    P.mm(pp[:, 0:512], C.xn[:, kc, c0:c0 + 128], wt[:, kc, 0:512], kc == 0, kc == 7, reads=[wk] + xk,
                         writes=[pk])
                st, stk = _stg(C)
                P.act(st[:, 0:512], pp[:, 0:512], AF.Copy, reads=[pk], writes=[stk])
                r0 = 1024 * gi + 128 * m
                P.dma("sp", V[r0:r0 + 128, 512 * vb:512 * (vb + 1)], st[:, 0:512], reads=[stk], writes=())
        for ti in GROUPS[gi]:
            s, e = TILES[ti]
            n = e - s
            so = max(s, PAD)
            xk = _xnk(ti, halo=False)
            pp, pk = _ps(C)
            for kc in range(8):
                P.mm(pp[0:8, 0:n], wf[:, kc, :], C.xn[:, kc, s:e], kc == 0, kc == 7, reads=["const"] + xk, writes=[pk])
            y, yk = _lf(C)
            P.act(y[:, 0:n], pp[0:8, 0:n], AF.Identity, reads=[pk, "const"], writes=[yk], bias=fb[:, 0:1], scale=1.0)
            ay, ayk = _lf(C)
            P.act(ay[:, 0:n], y[:, 0:n], AF.Abs, reads=[yk], writes=[ayk])
            P.act(ay[:, 0:n], ay[:, 0:n], AF.Exp, reads=[ayk], writes=[ayk], scale=-1.0)
            P.act(ay[:, 0:n], ay[:, 0:n], AF.Ln, reads=[ayk], writes=[ayk], bias=C.one1[0:8, 0:1], scale=1.0)
            P.ts(y[:, 0:n], y[:, 0:n], 0.0, None, ALU.min, None, reads=[yk], writes=[yk])
            P.tt(y[:, 0:n], y[:, 0:n], ay[:, 0:n], ALU.subtract, reads=[yk, ayk], writes=[yk])
            P.dma("sp", LF[0:8, so - PAD:e - PAD], y[:, so - s:n], reads=[yk], writes=())


def build_T(mode, upto=9):
    nc = bass.Bass("TRN2", target_bir_lowering=False)
    din = lambda name, shape, dt=F32: nc.dram_tensor(name, shape, dt, kind="ExternalInput").ap()
    dout = lambda name, shape, dt=F32: nc.dram_tensor(name, shape, dt, kind="ExternalOutput").ap()
    hin = din("hin", [D, NCOL])
    consts = din("cst_bf", [128, 256], BF16)
    if mode in (2, 3):
        oT = din("oT", [D, NCOL], BF16)
        w_o = din("w_o", [D, D])
        ffn = []
        for j in range(2):
            ffn.append(dict(g=din("ffn_g%d" % j, [128, 8]), up=din("ffn_up%d" % j, [D, 2 * DFF]),
                            cw=din("ffn_cw%d" % j, [128, 44, 3]), dn=din("ffn_dn%d" % j, [DFF, D])))
        cv = dict(g=din("cv_g", [128, 8]), w_in=din("cv_in", [D, 3 * D]), ck=din("cv_k", [128, 8, 3]),
                  w_out=din("cv_out", [D, D]))
        hout = dout("hout", [D, NTOK])
    if mode in (1, 2):
        at = dict(g=din("at_g", [128, 8]), w_in=din("at_in", [D, 3080]), gqk=din("at_gqk", [128, 16]),
                  fb=din("at_fb", [8, 1]))
        QT = dout("QT", [D, NTOK], BF16)
        KT = dout("KT", [D, NTOK], BF16)
        V = dout("V", [NTOK, D], BF16)
        LF = dout("LF", [8, NTOK])

    with ExitStack() as stack:
        P = Prog(nc, stack)
        C = t_alloc(nc, stack, P)
        sb = lambda name, shape, dt: stack.enter_context(nc.sbuf_tensor(name, shape, dt))
        C.eps1 = sb("eps1", [128, 1], F32)
        C.eps64 = sb("eps64", [128, 1], F32)
        C.one1 = sb("one1", [128, 1], F32)
        P.memset("dve", C.eps1[:, :], EPS, ["const"])
        P.memset("dve", C.eps64[:, :], 64 * EPS, ["const"])
        P.memset("dve", C.one1[:, :], 1.0, ["const"])
        P.dma("sp", C.ones[:, :], consts[:, 0:128], writes=["const"], semkey="c0")
        P.dma("sp", C.bd[:, :], consts[:, 128:256], writes=["const"], semkey="c1")

        def small(name, ap, shape, dt=F32, q="sp"):
            t = sb(name, shape, dt)
            P.dma(q, t[tuple(slice(None) for _ in shape)], ap, writes=["const"], semkey="c_" + name)
            return t

        if mode in (2, 3):
            for j in range(2):
                ffn[j]["g_sb"] = small("ffn_g_sb%d" % j, ffn[j]["g"], [128, 8])
                ffn[j]["cw_sb"] = small("ffn_cw_sb%d" % j, ffn[j]["cw"], [128, 44, 3])
            cv["g_sb"] = small("cv_g_sb", cv["g"], [128, 8])
            cv["ck_sb"] = small("cv_k_sb", cv["ck"], [128, 8, 3])
        if mode in (1, 2):
            at["g_sb"] = small("at_g_sb", at["g"], [128, 8])
            at["gqk_sb"] = small("at_gqk_sb", at["gqk"], [128, 16])
            at["fb_sb"] = small("at_fb_sb", at["fb"], [8, 1])
            at["wf_sb"] = small("at_wf_sb", at["w_in"].rearrange("(k p) c -> p k c", p=128)[:, :, 3072:3080],
                                [128, 8, 8], BF16, q="pool")
        for c in range(8):
            P.dma("sp", C.hT[:, c, :], hin[128 * c:128 * (c + 1), :], writes=[("h", c, ti) for ti in range(6)],
                  semkey=("hload", c))
        if mode in (2, 3):
            t_outproj(C, oT, w_o)
            if upto >= 2:
                t_ffn(C, ffn[0]["g_sb"], ffn[0]["up"], ffn[0]["dn"], ffn[0]["cw_sb"])
            if upto >= 3:
                t_convmix(C, cv["g_sb"], cv["w_in"], cv["ck_sb"], cv["w_out"])
            if upto >= 4:
                t_ffn(C, ffn[1]["g_sb"], ffn[1]["up"], ffn[1]["dn"], ffn[1]["cw_sb"])
            for c in range(8):
                P.dma("sp", hout[128 * c:128 * (c + 1), :], C.hT[:, c, PAD:NCOL], reads=[("h", c, ti) for ti in range(6)],
                      semkey=("hstore", c))
        if mode in (1, 2):
            t_qkv(C, at["g_sb"], at["w_in"], at["gqk_sb"], at["fb_sb"], at["wf_sb"], QT, KT, V, LF)
        P.emit()
    return nc


QT_W = 512
NQT = S // QT_W
NKB = S // 128
FCH = 512
NFC = S // FCH
ND_WARM = 1
D2, D3 = 2, 4


def build_A():
    nc = bass.Bass("TRN2", target_bir_lowering=False)
    din = lambda name, shape, dt=F32: nc.dram_tensor(name, shape, dt, kind="ExternalInput").ap()
    qf = din("qf", [128, S], BF16)
    kf = din("kf", [128, S], BF16)
    vf = din("vf", [S, 128], BF16)
    lf = din("lf", [2, S])
    qs = din("qs", [128, S], BF16)
    ks = din("ks", [128, S], BF16)
    vs = din("vs", [S, 128], BF16)
    cbf = din("acst_bf", [128, 256], BF16)
    cf32 = din("acst_f32", [128, 2 * 4 * 512])
    ctri = din("acst_tri", [32, 32])
    oT = nc.dram_tensor("oT", [256, S], BF16, kind="ExternalOutput").ap()

    with ExitStack() as stack:
        P = Prog(nc, stack)
        sb = lambda name, shape, dt: stack.enter_context(nc.sbuf_tensor(name, shape, dt))
        Qa = [sb("Qa%d" % i, [128, S], BF16) for i in range(2)]
        Ka = [sb("Ka%d" % i, [128, S], BF16) for i in range(2)]
        Va = [sb("Va%d" % i, [128, NKB, 65], BF16) for i in range(2)]
        Qs = sb("Qs", [128, S], BF16)
        Ks = sb("Ks", [128, S], BF16)
        Vs = sb("Vs", [128, NKB, 128], BF16)
        negtri = sb("negtri", [128, 128], BF16)
        negones = sb("negones", [128, 128], BF16)
        masks = sb("masks", [128, 8, 512], F32)
        onesf = sb("onesf", [128, 512], F32)
        tri32 = sb("tri32", [32, 32], F32)
        offs = sb("offs", [32, 2], F32)
        NPM, NTMF, NE1, NSP, NSS, NEG, NWT = 4, 2, 6, 5, 6, 6, 4
        pm = [sb("pm%d" % i, [128, 512], BF16) for i in range(NPM)]
        tmf = [sb("tmf%d" % i, [128, 512], F32) for i in range(NTMF)]
        ex = [sb("ex_%d" % i, [128, 2, 512], F32) for i in range(NE1)]
        e1 = [t[:, 0, :] for t in ex]
        eG = [t[:, 1, :] for t in ex]
        sp = [sb("sp_%d" % i, [128, 512], BF16) for i in range(NSP)]
        ssum = [sb("ssum_%d" % i, [128, 512], BF16) for i in range(NSS)]
        wt = [sb("wt_%d" % i, [128, 512], BF16) for i in range(NWT)]
        den = sb("den", [128, 512], F32)
        rden = sb("rden", [64, 512], F32)
        ost = [sb("ost%d" % i, [64, 512], BF16) for i in range(4)]
        psall = stack.enter_context(nc.psum_tensor("psall", [128, 8, 512], F32))
        ps = [psall[:, i, :] for i in range(8)]

        P.dma("sp", negtri[:, :], cbf[:, 0:128], writes=["c_tri"])
        P.dma("sp", negones[:, :], cbf[:, 128:256], writes=["c_ones"])
        P.dma("sp", tri32[:, :], ctri[:, :], writes=["c_tri32"])
        P.memset("dve", onesf[:, :], 1.0, ["c_onesf"])
        dummy = sb("warm_rhs", [128, 512], BF16)
        P.memset("dve", dummy[:, :], 0.0, ["c_dummy"])

        vvf = vf.rearrange("(b p) d -> p b d", p=128)
        for hh in range(2):
            P.dma("sp", Qa[hh][0:64, :], qf[64 * hh:64 * hh + 64, :], writes=[("Qf", hh)])
            P.dma("sp", Ka[hh][0:64, :], kf[64 * hh:64 * hh + 64, :], writes=[("Kf", hh)])
            for half in range(2):
                P.dma("sp", Va[hh][:, 32 * half:32 * half + 32, 0:64], vvf[:, 32 * half:32 * half + 32, 64 * hh:64 * hh + 64],
                      writes=[("Vf", hh)], semkey=("Vf", hh))
            P.memset("dve", Va[hh][:, :, 64:65], 1.0, [("V1", hh)])
            P.memset("dve", Qa[hh][64:70, :], 1.0, [("Qx", hh)])
            P.memset("dve", Ka[hh][64:70, :], 1.0, [("Kx", hh)])
            if hh == 0:
                P.dma("sp", Qs[:, :], qs[:, :], writes=["Qs"])
                P.dma("sp", Ks[:, :], ks[:, :], writes=["Ks"])
                vvs = vs.rearrange("(b p) d -> p b d", p=128)
                for half in range(2):
                    P.dma("sp", Vs[:, 32 * half:32 * half + 32, :], vvs[:, 32 * half:32 * half + 32, :], writes=["Vs"],
                          semkey="Vs")
        P.dma("sp", masks[:, :, :], cf32.rearrange("p (a b) -> p a b", a=8), writes=["c_masks"])

        R = 2 * NFC
        lf32, lfk = e1[0][0:R, :], ("e1", 0)
        loc, lock = eG[0][0:R, :], ("eG", 0)
        Ft, Fk = tmf[0][0:R, :], ("tmf", 0)
        r1, r1k = tmf[1][0:R, :], ("tmf", 1)
        r2, r2k = e1[1][0:R, :], ("e1", 1)
        P.dma("sp", lf32, lf.rearrange("h (c n) -> (h c) n", n=FCH), writes=[lfk])
        P.op("dve", lambda e: e.tensor_tensor_scan(out=loc, data0=onesf[0:R, :], data1=lf32, initial=0.0,
                                                  op0=ALU.mult, op1=ALU.add), reads=[lfk, "c_onesf"], writes=[lock])
        P.mm(ps[4][0:R, 0:1], tri32[:, :], loc[:, FCH - 1:FCH], True, True, reads=[lock, "c_tri32"], writes=[("ps", 4)])
        P.act(offs[:, 0:1], ps[4][0:R, 0:1], AF.Copy, reads=[("ps", 4)], writes=["offs"])
        P.ts(Ft, loc, offs[:, 0:1], None, ALU.add, None, reads=[lock, "offs"], writes=[Fk])
        hi, hik = sp[0][0:R, :], ("sp", 0)
        mid, midk = sp[1][0:R, :], ("sp", 1)
        lo, lok = sp[2][0:R, :], ("sp", 2)
        P.copy("dve", hi, Ft, reads=[Fk], writes=[hik])
        P.tt(r1, Ft, hi, ALU.subtract, reads=[Fk, hik], writes=[r1k])
        P.copy("dve", mid, r1, reads=[r1k], writes=[midk])
        P.tt(r2, r1, mid, ALU.subtract, reads=[r1k, midk], writes=[r2k])
        P.copy("dve", lo, r2, reads=[r2k], writes=[lok])
        negs = []
        for j, (b, bk) in enumerate(((hi, hik), (mid, midk), (lo, lok))):
            nb_, nbk = wt[j][0:R, :], ("wt", j)
            P.ts(nb_, b, -1.0, None, ALU.mult, None, reads=[bk], writes=[nbk], eng="pool")
            negs.append((nb_, nbk))
        for hh in range(2):
            for j, ((b, bk), (nb_, nbk)) in enumerate(zip(((hi, hik), (mid, midk), (lo, lok)), negs)):
                for ch in range(NFC):
                    r = NFC * hh + ch
                    P.dma("sp", Qa[hh][64 + j:65 + j, ch * FCH:(ch + 1) * FCH], b[r:r + 1, :], reads=[bk, ("Qx", hh)],
                          writes=[("Qg", hh)], semkey=("Qg", hh))
                    P.dma("sp", Ka[hh][67 + j:68 + j, ch * FCH:(ch + 1) * FCH], nb_[r:r + 1, :], reads=[nbk, ("Kx", hh)],
                          writes=[("Kg", hh)], semkey=("Kg", hh))

        cnt = {"pm": 0, "tmf": 0, "e1": 0, "sp": 0, "ss": 0, "eG": 0, "wt": 0, "Ss": 0, "Sf": 0, "G": 0, "ost": 0}

        def nxt(name, n):
            i = cnt[name]
            cnt[name] = i + 1
            return i % n

        for pair in range(2):
            fq = [("Qf", pair), ("Qg", pair), ("Qx", pair)]
            fk = [("Kf", pair), ("Kg", pair), ("Kx", pair)]
            fv = [("Vf", pair), ("V1", pair)]
            r0 = 64 * pair
            sbq = Qs[r0:r0 + 64, :]
            sbk = Ks[r0:r0 + 64, :]
            fblocks = [(qt, kb) for qt in range(NQT) for kb in range(4 * (qt + 1))]
            sblocks = [(qt, kb) for qt in range(NQT) for kb in range(4 * (qt + 1) - 1, -1, -1)]
            NBL = len(fblocks)
            fst = {}
            sst = {}
            pending = []
            lastG = [1]

            def fox_qk(g):
                qt, kb = fblocks[g]
                si = 4 + nxt("Sf", 2)
                fst[g] = dict(S=si)
                P.mm(ps[si][:, :], Ka[pair][0:70, kb * 128:(kb + 1) * 128], Qa[pair][0:70, qt * QT_W:(qt + 1) * QT_W],
                     True, True, reads=fq + fk, writes=[("ps", si)])

            def fox_exp(g):
                qt, kb = fblocks[g]
                si = fst[g]["S"]
                j = kb - 4 * qt
                pi = nxt("pm", NPM)
                fst[g]["pm"] = pi
                if j >= 0:
                    ti = nxt("tmf", NTMF)
                    P.tt(tmf[ti][:, :], ps[si][:, :], masks[:, j, :], ALU.add, reads=[("ps", si), "c_masks"],
                         writes=[("tmf", ti)])
                    P.act(pm[pi][:, :], tmf[ti][:, :], AF.Exp, reads=[("tmf", ti)], writes=[("pm", pi)])
                else:
                    P.act(pm[pi][:, :], ps[si][:, :], AF.Exp, reads=[("ps", si)], writes=[("pm", pi)])

            def fox_pv(g):
                qt, kb = fblocks[g]
                nkb = 4 * (qt + 1)
                pi = fst.pop(g)["pm"]
                P.mm(ps[6][0:65, :], Va[pair][:, kb, 0:65], pm[pi][:, :], kb == 0, kb == nkb - 1,
                     reads=fv + [("pm", pi)], writes=[("ps", 6)])
                if kb == nkb - 1:
                    pending.append(qt)

            def fox_epilogue(qt):
                q0 = qt * QT_W
                gi = lastG[0]
                P.act(den[64:65, :], ps[6][64:65, :], AF.Copy, reads=[("ps", 6)], writes=["den"])
                P.mm(ps[gi][0:64, :], onesf[64:65, 0:64], den[64:65, :], True, True, reads=["den", "c_onesf"],
                     writes=[("ps", gi)])
                P.recip(rden[:, :], ps[gi][0:64, :], reads=[("ps", gi)], writes=["rden"])
                oi = nxt("ost", 4)
                P.tt(ost[oi][:, :], ps[6][0:64, :], rden[:, :], ALU.mult, reads=[("ps", 6), "rden"],
                     writes=[("ost", oi)])
                P.dma("sp", oT[r0:r0 + 64, q0:q0 + QT_W], ost[oi][:, :], reads=[("ost", oi)])

            def sb_qk(g):
                qt, kb = sblocks[g]
                si = 2 * (g % 2)
                sst.setdefault(g, {})["S"] = si
                P.mm(ps[si][:, :], sbk[:, kb * 128:(kb + 1) * 128], sbq[:, qt * QT_W:(qt + 1) * QT_W], True, True,
                     reads=["Qs", "Ks"], writes=[("ps", si)])

            def sb_exp(g, merged):
                qt, kb = sblocks[g]
                d = sst[g]
                si = d["S"]
                ei = g % NE1
                d["e1"] = ei
                if merged:
                    P.act(ex[ei][:, :, :], psall[:, si:si + 2, :], AF.Exp, reads=[("ps", si), ("ps", si + 1)],
                          writes=[("e1", ei), ("eG", ei)])
                else:
                    P.act(e1[ei][:, :], ps[si][:, :], AF.Exp, reads=[("ps", si)], writes=[("e1", ei)])
                j = kb - 4 * qt
                if j >= 0:
                    P.tt(e1[ei][:, :], e1[ei][:, :], masks[:, 4 + j, :], ALU.mult, reads=[("e1", ei), "c_masks"],
                         writes=[("e1", ei)], eng="dve")

            def sb_ln(g):
                qt, kb = sblocks[g]
                nkb = 4 * (qt + 1)
                k = nkb - 1 - kb
                d = sst[g]
                ei = d["e1"]
                spi = nxt("sp", NSP)
                d["sp"] = spi
                P.act(sp[spi][:, :], e1[ei][:, :], AF.Ln, reads=[("e1", ei), "c_onesf"], writes=[("sp", spi)],
                      bias=onesf[:, 0:1], scale=1.0)
                if k + 1 < nkb:
                    nsi = nxt("ss", NSS)
                    sst.setdefault(g + 1, {})["ss"] = nsi
                    if k == 0:
                        P.copy("pool", ssum[nsi][:, :], sp[spi][:, :], reads=[("sp", spi)], writes=[("ss", nsi)])
                    else:
                        ci = d["ss"]
                        P.tt(ssum[nsi][:, :], ssum[ci][:, :], sp[spi][:, :], ALU.add,
                             reads=[("ss", ci), ("sp", spi)], writes=[("ss", nsi)], eng="pool")

            def sb_s2a(g):
                qt, kb = sblocks[g]
                nkb = 4 * (qt + 1)
                k = nkb - 1 - kb
                d = sst[g]
                gi = 2 * ((g + 1) % 2) + 1
                d["G"] = gi
                spi = d["sp"]
                P.mm(ps[gi][:, :], negtri[:, :], sp[spi][:, :], True, k == 0, reads=[("sp", spi), "c_tri"],
                     writes=[("ps", gi)])
                if k > 0:
                    ci = d["ss"]
                    P.mm(ps[gi][:, :], negones[:, :], ssum[ci][:, :], False, True, reads=[("ss", ci), "c_ones"],
                         writes=[("ps", gi)])

            def sb_s2b(g, merged):
                d = sst[g]
                gi = d["G"]
                lastG[0] = gi
                gi2 = (g + 3) % NE1
                if not merged:
                    P.act(eG[gi2][:, :], ps[gi][:, :], AF.Exp, reads=[("ps", gi)], writes=[("eG", gi2)])
                wi = nxt("wt", NWT)
                d["wt"] = wi
                P.tt(wt[wi][:, :], e1[d["e1"]][:, :], eG[gi2][:, :], ALU.mult, reads=[("e1", d["e1"]), ("eG", gi2)],
                     writes=[("wt", wi)])

            def sb_s3(g):
                qt, kb = sblocks[g]
                nkb = 4 * (qt + 1)
                k = nkb - 1 - kb
                d = sst.pop(g)
                wi = d["wt"]
                P.mm(ps[7][0:64, :], Vs[:, kb, r0:r0 + 64], wt[wi][:, :], k == 0, k == nkb - 1,
                     reads=["Vs", ("wt", wi)], writes=[("ps", 7)])
                if k == nkb - 1:
                    q0 = qt * QT_W
                    oi = nxt("ost", 4)
                    P.act(ost[oi][:, :], ps[7][0:64, :], AF.Copy, reads=[("ps", 7)], writes=[("ost", oi)])
                    P.dma("sp", oT[128 + r0:128 + r0 + 64, q0:q0 + QT_W], ost[oi][:, :], reads=[("ost", oi)])

            ok = lambda x: 0 <= x < NBL
            for g in range(-1, NBL + 5):
                if ok(g + 1):
                    sb_qk(g + 1)
                if ok(g):
                    fox_qk(g)
                if ok(g - 2):
                    fox_pv(g - 2)
                    sb_s2a(g - 2)
                if ok(g - 4):
                    sb_s3(g - 4)
                for _ in range(ND_WARM):
                    P.mm(ps[7][64:128, :], negones[:, 0:64], dummy[:, :], True, True, reads=["c_ones", "c_dummy"], writes=())
                merged = ok(g) and ok(g - 3)
                if ok(g):
                    sb_exp(g, merged)
                if ok(g - 1):
                    fox_exp(g - 1)
                if ok(g):
                    sb_ln(g)
                if ok(g - 3):
                    sb_s2b(g - 3, merged)
                while pending:
                    fox_epilogue(pending.pop(0))
        P.emit()
    return nc


def _bf(a):
    return np.ascontiguousarray(a).astype(ml_dtypes.bfloat16)


def _cols(vec, nchunk):
    return np.ascontiguousarray(np.asarray(vec, np.float32).reshape(nchunk, 128).T)


def _taps(w, nchunk):
    return np.ascontiguousarray(np.asarray(w, np.float32).T.reshape(nchunk, 128, 3).transpose(1, 0, 2))


def _t_consts():
    ones = np.ones((128, 128), np.float32)
    bd = np.zeros((128, 128), np.float32)
    bd[:64, :64] = 1.0
    bd[64:, 64:] = 1.0
    return _bf(np.concatenate([ones, bd], axis=1))


def _a_consts():
    j = np.arange(128)[:, None]
    s = np.arange(128)[None, :]
    negtri = np.where(j >= s, -1.0, 0.0).astype(np.float32)
    negones = -np.ones((128, 128), np.float32)
    cbf = _bf(np.concatenate([negtri, negones], axis=1))
    sl = np.arange(128)[:, None]
    tl = np.arange(512)[None, :]
    madd = np.stack([np.where(128 * jj + sl <= tl, 0.0, -30000.0) for jj in range(4)], axis=1)
    m01 = np.stack([np.where(128 * jj + sl < tl, 1.0, 0.0) for jj in range(4)], axis=1)
    cf = np.concatenate([madd, m01], axis=1).astype(np.float32).reshape(128, 8 * 512)
    pp = np.arange(32)
    tri = ((pp[:, None] < pp[None, :]) & (pp[:, None] // 16 == pp[None, :] // 16)).astype(np.float32)
    return cbf, (np.ascontiguousarray(cf), np.ascontiguousarray(tri))


def _pad_cols(full, j, dtype):
    out = np.zeros((full.shape[0], NCOL), dtype)
    t0 = NTOK * j
    out[:, PAD:] = full[:, t0:t0 + NTOK]
    if j > 0:
        out[:, 2:PAD] = full[:, t0 - 6:t0]
    return out


def _attn_inputs(inp, i):
    gq = np.concatenate([np.tile(inp["fox_q_gain"][i], 8), np.tile(inp["sb_q_gain"][i], 8)])
    gk = np.concatenate([np.tile(inp["fox_k_gain"][i], 8), np.tile(inp["sb_k_gain"][i], 8)])
    return {
        "at_g": _cols(inp["attn_norm"][i], 8),
        "at_in": np.ascontiguousarray(inp["attn_w_in"][i], dtype=np.float32),
        "at_gqk": np.ascontiguousarray(np.concatenate([_cols(gq, 8), _cols(gk, 8)], axis=1)),
        "at_fb": np.ascontiguousarray(np.asarray(inp["attn_f_bias"][i], np.float32).reshape(8, 1)),
    }


def _block_inputs(inp, i):
    d = {"w_o": np.ascontiguousarray(inp["attn_w_out"][i], dtype=np.float32)}
    for jj, l in enumerate((2 * i, 2 * i + 1)):
        d["ffn_g%d" % jj] = _cols(inp["ffn_norm"][l], 8)
        d["ffn_up%d" % jj] = np.ascontiguousarray(inp["ffn_w_up"][l], dtype=np.float32)
        d["ffn_cw%d" % jj] = _taps(inp["ffn_conv"][l], 44)
        d["ffn_dn%d" % jj] = np.ascontiguousarray(inp["ffn_w_down"][l], dtype=np.float32)
    d["cv_g"] = _cols(inp["conv_norm"][i], 8)
    d["cv_in"] = np.ascontiguousarray(inp["conv_w_in"][i], dtype=np.float32)
    d["cv_k"] = _taps(inp["conv_kernel"][i], 8)
    d["cv_out"] = np.ascontiguousarray(inp["conv_w_out"][i], dtype=np.float32)
    return d


def _run(nc, in_maps):
    res = run_bass_kernel_spmd(nc, in_maps, core_ids=list(range(8)))
    return res.results


def _gather_fm(results, name):
    return [np.concatenate([np.asarray(results[4 * b + j][name]) for j in range(4)], axis=1) for b in range(NB)]


def _attention(results, cbf, cf):
    QT = _gather_fm(results, "QT")
    KT = _gather_fm(results, "KT")
    LF = _gather_fm(results, "LF")
    V = [np.concatenate([np.asarray(results[4 * b + j]["V"]) for j in range(4)], axis=0) for b in range(NB)]
    maps = []
    for c in range(8):
        b, g = divmod(c, 4)
        f0, s0 = 128 * g, 512 + 128 * g
        maps.append({
            "qf": np.ascontiguousarray(QT[b][f0:f0 + 128]), "kf": np.ascontiguousarray(KT[b][f0:f0 + 128]),
            "vf": np.ascontiguousarray(V[b][:, f0:f0 + 128]), "lf": np.ascontiguousarray(LF[b][2 * g:2 * g + 2]),
            "qs": np.ascontiguousarray(QT[b][s0:s0 + 128]), "ks": np.ascontiguousarray(KT[b][s0:s0 + 128]),
            "vs": np.ascontiguousarray(V[b][:, s0:s0 + 128]), "acst_bf": cbf, "acst_f32": cf[0], "acst_tri": cf[1],
        })
    ares = _run(build_A(), maps)
    oT = []
    for b in range(NB):
        o = np.zeros((D, S), ml_dtypes.bfloat16)
        for g in range(4):
            r = np.asarray(ares[4 * b + g]["oT"])
            o[128 * g:128 * g + 128] = r[0:128]
            o[512 + 128 * g:512 + 128 * g + 128] = r[128:256]
        oT.append(o)
    return oT


def kernel(**inp):
    inp = {k: np.asarray(v) for k, v in inp.items()}
    x = inp["x"].astype(np.float32, copy=False)
    tc = _t_consts()
    cbf, cf = _a_consts()
    hfull = [np.ascontiguousarray(x[b].T) for b in range(NB)]
    maps = []
    for c in range(8):
        b, j = divmod(c, 4)
        m = {"hin": _pad_cols(hfull[b], j, np.float32), "cst_bf": tc}
        m.update(_attn_inputs(inp, 0))
        maps.append(m)
    r1 = _run(build_T(1), maps)
    oT = _attention(r1, cbf, cf)
    maps = []
    for c in range(8):
        b, j = divmod(c, 4)
        m = {"hin": _pad_cols(hfull[b], j, np.float32), "oT": _pad_cols(oT[b], j, ml_dtypes.bfloat16), "cst_bf": tc}
        m.update(_block_inputs(inp, 0))
        m.update(_attn_inputs(inp, 1))
        maps.append(m)
    r2 = _run(build_T(2), maps)
    hfull = _gather_fm(r2, "hout")
    oT = _attention(r2, cbf, cf)
    maps = []
    for c in range(8):
        b, j = divmod(c, 4)
        m = {"hin": _pad_cols(hfull[b], j, np.float32), "oT": _pad_cols(oT[b], j, ml_dtypes.bfloat16), "cst_bf": tc}
        m.update(_block_inputs(inp, 1))
        maps.append(m)
    r3 = _run(build_T(3), maps)
    hfull = _gather_fm(r3, "hout")
    out = np.stack([np.ascontiguousarray(hfull[b].T) for b in range(NB)], axis=0)
    return out.astype(np.float32)
```
